# Optimizing a Trainium2 kernel written in Bass

```python
import math
import jax, jax.numpy as jnp
from jax import lax
import numpy as np

D_MODEL = 1024
BATCH = 8
SEQ = 2048
DEPTH = 2
DEC_BATCH = 128
DEC_SEQ = 4
PAST_LEN = 16384
PAGE_SIZE = 128

N_HEADS = 4
HEAD_DIM = 128
MIX_W = N_HEADS * HEAD_DIM
N_BRANCH = 3
CONV_W = 4
FFN_CONV_W = 3
D_FF = 2816
CHUNK = 64
EPS = 1e-6
IN_SPLITS = (3 * MIX_W, MIX_W, N_HEADS, N_HEADS,
             MIX_W, MIX_W, MIX_W, MIX_W, N_HEADS, N_HEADS,
             MIX_W, MIX_W, MIX_W, MIX_W,
             N_BRANCH * D_MODEL)
N_IN = sum(IN_SPLITS)

kernel_name = "gdn_mlstm_hgrn2_parallel_hybrid_step"


def _rmsnorm(x, g):
    xf = x.astype(jnp.float32)
    return (xf * lax.rsqrt(jnp.mean(xf * xf, -1, keepdims=True) + EPS) * g).astype(x.dtype)


def _group_rmsnorm(t, g):
    B, L, _ = t.shape
    th = t.reshape(B, L, N_HEADS, HEAD_DIM)
    th = th * lax.rsqrt(jnp.mean(th * th, -1, keepdims=True) + EPS)
    return th.reshape(B, L, MIX_W) * g.astype(jnp.float32)


def _l2norm(t):
    return t * lax.rsqrt(jnp.sum(t * t, -1, keepdims=True) + EPS)


def _heads(t):
    B, L, _ = t.shape
    return t.astype(jnp.float32).reshape(B, L, N_HEADS, HEAD_DIM).transpose(0, 2, 1, 3)


def _merge(t):
    B, H, L, D = t.shape
    return t.transpose(0, 2, 1, 3).reshape(B, L, H * D)


def _chunk_len(L):
    return CHUNK if L % CHUNK == 0 else L


def _to_chunks(t, c):
    B, H, L = t.shape[:3]
    return jnp.moveaxis(t.reshape(B, H, L // c, c, *t.shape[3:]), 2, 0)


def _from_chunks(t):
    Nc, B, H, c = t.shape[:4]
    return jnp.moveaxis(t, 0, 2).reshape(B, H, Nc * c, *t.shape[4:])


def _causal_conv(x, buf, w):
    W = w.shape[0]
    L = x.shape[1]
    xp = jnp.concatenate([buf.astype(x.dtype), x], axis=1)
    y = sum(w[j] * xp[:, j:j + L] for j in range(W))
    return y, xp[:, L:]


def _gated_delta(q, k, v, beta, g, S0):
    c = _chunk_len(q.shape[2])
    causal = jnp.tril(jnp.ones((c, c), bool))
    strict = jnp.tril(jnp.ones((c, c), bool), -1)
    eye = jnp.eye(c, dtype=jnp.float32)

    def step(S, inp):
        qc, kc, vc, bc, gc = inp
        G = jnp.cumsum(gc, -1)
        decay = jnp.exp(jnp.where(causal, G[..., :, None] - G[..., None, :], -jnp.inf))
        A = jnp.where(strict, bc[..., None] * decay * jnp.einsum('bhtd,bhsd->bhts', kc, kc), 0.0)
        eG = jnp.exp(G)[..., None]
        rhs = bc[..., None] * (vc - eG * jnp.einsum('bhtd,bhde->bhte', kc, S))
        u = lax.linalg.triangular_solve(eye + A, rhs, left_side=True, lower=True)
        qk = jnp.einsum('bhtd,bhsd->bhts', qc, kc) * decay
        o = eG * jnp.einsum('bhtd,bhde->bhte', qc, S) + jnp.einsum('bhts,bhse->bhte', qk, u)
        wl = jnp.exp(G[..., -1:] - G)[..., None]
        S_new = jnp.exp(G[..., -1])[..., None, None] * S + jnp.einsum('bhsd,bhse->bhde', kc * wl, u)
        return S_new, o

    S, o = lax.scan(step, S0, tuple(_to_chunks(t, c) for t in (q, k, v, beta, g)))
    return _from_chunks(o), S


def _mlstm(q, k, v, ig, lf, C0, n0, m0):
    c = _chunk_len(q.shape[2])
    causal = jnp.tril(jnp.ones((c, c), bool))

    def step(carry, inp):
        C, n, m = carry
        qc, kc, vc, ic, fc = inp
        F = jnp.cumsum(fc, -1)
        logD = jnp.where(causal, F[..., :, None] - F[..., None, :] + ic[..., None, :], -jnp.inf)
        b = F + m[..., None]
        mt = jnp.maximum(b, jnp.max(logD, -1))
        s = jnp.einsum('bhtd,bhsd->bhts', qc, kc) * jnp.exp(logD - mt[..., None])
        inter = jnp.exp(b - mt)
        num = jnp.einsum('bhts,bhse->bhte', s, vc) + inter[..., None] * jnp.einsum('bhtd,bhde->bhte', qc, C)
        den = jnp.sum(s, -1) + inter * jnp.einsum('bhtd,bhd->bht', qc, n)
        h = num / jnp.maximum(jnp.abs(den), jnp.exp(-mt))[..., None]
        m_new = mt[..., -1]
        wl = jnp.exp(F[..., -1:] - F + ic - m_new[..., None])
        d0 = jnp.exp(F[..., -1] + m - m_new)
        C_new = d0[..., None, None] * C + jnp.einsum('bhsd,bhse->bhde', kc * wl[..., None], vc)
        n_new = d0[..., None] * n + jnp.einsum('bhs,bhsd->bhd', wl, kc)
        return (C_new, n_new, m_new), h

    state, h = lax.scan(step, (C0, n0, m0), tuple(_to_chunks(t, c) for t in (q, k, v, ig, lf)))
    return _from_chunks(h), state


def _hgrn2(q, k, lg, i, S0):
    c = _chunk_len(q.shape[2])
    causal = jnp.tril(jnp.ones((c, c), bool))[..., None]

    def step(S, inp):
        qc, kc, lgc, ic = inp
        G = jnp.cumsum(lgc, axis=2)
        decay = jnp.exp(jnp.where(causal, G[:, :, :, None, :] - G[:, :, None, :, :], -jnp.inf))
        A = jnp.einsum('bhtsd,bhsd->bhts', qc[:, :, :, None, :] * decay, kc)
        o = jnp.einsum('bhtd,bhde->bhte', qc * jnp.exp(G), S) + jnp.einsum('bhts,bhse->bhte', A, ic)
        S_new = jnp.exp(G[:, :, -1])[..., None] * S + jnp.einsum('bhsd,bhse->bhde', kc * jnp.exp(G[:, :, -1:] - G), ic)
        return S_new, o

    S, o = lax.scan(step, S0, tuple(_to_chunks(t, c) for t in (q, k, lg, i)))
    return _from_chunks(o), S


def _layer(x, gdn_S, gdn_conv, m_C, m_n, m_m, h_S, ffn_conv,
           ln_mix, w_in, gdn_conv_w, gdn_A_log, gdn_dt_bias, gdn_norm,
           m_ibias, m_fbias, m_norm, lb, hgrn_norm, w_br, w_out,
           ln_ffn, w_up, ffn_conv_w, ffn_conv_b, w_down):
    f32 = jnp.float32
    B, L, _ = x.shape
    h = _rmsnorm(x, ln_mix)
    proj = h @ w_in
    (g_qkv, g_z, g_b, g_a, m_q, m_k, m_v, m_o, m_i, m_f,
     h_q, h_f, h_i, h_g, br_gate) = jnp.split(proj, np.cumsum(IN_SPLITS)[:-1].tolist(), axis=-1)

    g_qkv, gdn_conv_new = _causal_conv(g_qkv, gdn_conv, gdn_conv_w)
    gq, gk, gv = jnp.split(jax.nn.silu(g_qkv.astype(f32)), 3, axis=-1)
    gq = _l2norm(_heads(gq)) * HEAD_DIM ** -0.5
    gk = _l2norm(_heads(gk))
    beta = jax.nn.sigmoid(g_b.astype(f32)).transpose(0, 2, 1)
    g_log = (-jnp.exp(gdn_A_log.astype(f32)) * jax.nn.softplus(g_a.astype(f32) + gdn_dt_bias)).transpose(0, 2, 1)
    o, gdn_S_new = _gated_delta(gq, gk, _heads(gv), beta, g_log, gdn_S.astype(f32))
    o_gdn = _group_rmsnorm(_merge(o), gdn_norm) * jax.nn.silu(g_z.astype(f32))

    ig = (m_i.astype(f32) + m_ibias).transpose(0, 2, 1)
    lf = jax.nn.log_sigmoid(m_f.astype(f32) + m_fbias).transpose(0, 2, 1)
    hm, (m_C_new, m_n_new, m_m_new) = _mlstm(_heads(m_q), _heads(m_k) * HEAD_DIM ** -0.5, _heads(m_v),
                                             ig, lf, m_C.astype(f32), m_n.astype(f32), m_m.astype(f32))
    o_m = _group_rmsnorm(_merge(hm), m_norm) * jax.nn.sigmoid(m_o.astype(f32))

    fg = lb + (1.0 - lb) * jax.nn.sigmoid(h_f.astype(f32))
    ho, h_S_new = _hgrn2(_heads(jax.nn.silu(h_q.astype(f32))), _heads(1.0 - fg), _heads(jnp.log(fg)),
                         _heads(h_i), h_S.astype(f32))
    o_h = _group_rmsnorm(_merge(ho), hgrn_norm) * jax.nn.silu(h_g.astype(f32))

    outs = jnp.stack([o_gdn, o_m, o_h], axis=0).astype(x.dtype)
    br = jnp.einsum('nblc,ncd->blnd', outs, w_br)
    gate = jax.nn.sigmoid(br_gate.reshape(B, L, N_BRANCH, D_MODEL))
    x = x + (jnp.sum(gate * br, axis=2) @ w_out).astype(x.dtype)

    u = _rmsnorm(x, ln_ffn) @ w_up
    u, ffn_conv_new = _causal_conv(u, ffn_conv, ffn_conv_w)
    ua, ub = jnp.split(u + ffn_conv_b, 2, axis=-1)
    x = x + ((jax.nn.silu(ua) * ub) @ w_down).astype(x.dtype)

    dt = x.dtype
    new = (gdn_S_new.astype(dt), gdn_conv_new.astype(dt), m_C_new.astype(dt), m_n_new.astype(dt),
           m_m_new.astype(dt), h_S_new.astype(dt), ffn_conv_new.astype(dt))
    return x, new


def _trunk(x, gdn_S, gdn_conv, m_C, m_n, m_m, h_S, ffn_conv,
           ln_mix, w_in, gdn_conv_w, gdn_A_log, gdn_dt_bias, gdn_norm,
           m_ibias, m_fbias, m_norm, hgrn_lb, hgrn_norm, w_br, w_out,
           ln_ffn, w_up, ffn_conv_w, ffn_conv_b, w_down, ln_final):
    lb_all = jnp.cumsum(jax.nn.softmax(hgrn_lb.astype(jnp.float32), axis=0), axis=0)
    lb_all = lb_all - lb_all[0]
    per_layer = []
    for l in range(DEPTH):
        x, st = _layer(x, gdn_S[l], gdn_conv[l], m_C[l], m_n[l], m_m[l], h_S[l], ffn_conv[l],
                       ln_mix[l], w_in[l], gdn_conv_w[l], gdn_A_log[l], gdn_dt_bias[l], gdn_norm[l],
                       m_ibias[l], m_fbias[l], m_norm[l], lb_all[l], hgrn_norm[l], w_br[l], w_out[l],
                       ln_ffn[l], w_up[l], ffn_conv_w[l], ffn_conv_b[l], w_down[l])
        per_layer.append(st)
    stacked = tuple(jnp.stack(s, axis=0) for s in zip(*per_layer))
    return _rmsnorm(x, ln_final), stacked


def setup_inputs(seed: int = 0) -> dict:
    key = jax.random.key(seed)
    ks = jax.random.split(key, 32)
    f32 = jnp.float32
    H, Dh = N_HEADS, HEAD_DIM

    def nrm(k, shape, s):
        return jax.random.normal(k, shape, f32) * s

    dt = jnp.exp(jax.random.uniform(ks[12], (DEPTH, H), f32) * (math.log(0.1) - math.log(0.001)) + math.log(0.001))
    return {
        'x_prompt': nrm(ks[0], (BATCH, SEQ, D_MODEL), 1.0),
        'x_sample': nrm(ks[1], (DEC_BATCH, DEC_SEQ, D_MODEL), 1.0),
        'state_gdn_S': nrm(ks[2], (DEPTH, DEC_BATCH, H, Dh, Dh), 0.05),
        'state_gdn_conv': nrm(ks[3], (DEPTH, DEC_BATCH, CONV_W - 1, 3 * MIX_W), 1.0),
        'state_mlstm_C': nrm(ks[4], (DEPTH, DEC_BATCH, H, Dh, Dh), 0.05),
        'state_mlstm_n': nrm(ks[5], (DEPTH, DEC_BATCH, H, Dh), 0.1),
        'state_mlstm_m': nrm(ks[6], (DEPTH, DEC_BATCH, H), 1.0),
        'state_hgrn_S': nrm(ks[7], (DEPTH, DEC_BATCH, H, Dh, Dh), 0.5),
        'state_ffn_conv': nrm(ks[8], (DEPTH, DEC_BATCH, FFN_CONV_W - 1, 2 * D_FF), 1.0),
        'ln_mix': 1.0 + nrm(ks[9], (DEPTH, D_MODEL), 0.02),
        'w_in': nrm(ks[10], (DEPTH, D_MODEL, N_IN), D_MODEL ** -0.5),
        'gdn_conv_w': nrm(ks[11], (DEPTH, CONV_W, 3 * MIX_W), CONV_W ** -0.5),
        'gdn_A_log': jnp.log(jax.random.uniform(ks[13], (DEPTH, H), f32, 1.0, 16.0)),
        'gdn_dt_bias': dt + jnp.log(-jnp.expm1(-dt)),
        'gdn_norm': 1.0 + nrm(ks[14], (DEPTH, MIX_W), 0.02),
        'm_ibias': nrm(ks[15], (DEPTH, H), 0.1),
        'm_fbias': 3.0 + 3.0 * jax.random.uniform(ks[16], (DEPTH, H), f32),
        'm_norm': 1.0 + nrm(ks[17], (DEPTH, MIX_W), 0.02),
        'hgrn_lb': nrm(ks[18], (DEPTH, MIX_W), 0.1),
        'hgrn_norm': 1.0 + nrm(ks[19], (DEPTH, MIX_W), 0.02),
        'w_br': nrm(ks[20], (DEPTH, N_BRANCH, MIX_W, D_MODEL), MIX_W ** -0.5),
        'w_out': nrm(ks[21], (DEPTH, D_MODEL, D_MODEL), D_MODEL ** -0.5),
        'ln_ffn': 1.0 + nrm(ks[22], (DEPTH, D_MODEL), 0.02),
        'w_up': nrm(ks[23], (DEPTH, D_MODEL, 2 * D_FF), D_MODEL ** -0.5),
        'ffn_conv_w': nrm(ks[24], (DEPTH, FFN_CONV_W, 2 * D_FF), FFN_CONV_W ** -0.5),
        'ffn_conv_b': nrm(ks[25], (DEPTH, 2 * D_FF), 0.01),
        'w_down': nrm(ks[26], (DEPTH, D_FF, D_MODEL), D_FF ** -0.5),
        'ln_final': 1.0 + nrm(ks[27], (D_MODEL,), 0.02),
    }


def reference(x_prompt, x_sample, state_gdn_S, state_gdn_conv, state_mlstm_C, state_mlstm_n,
              state_mlstm_m, state_hgrn_S, state_ffn_conv, ln_mix, w_in, gdn_conv_w, gdn_A_log,
              gdn_dt_bias, gdn_norm, m_ibias, m_fbias, m_norm, hgrn_lb, hgrn_norm, w_br, w_out,
              ln_ffn, w_up, ffn_conv_w, ffn_conv_b, w_down, ln_final):
    weights = (ln_mix, w_in, gdn_conv_w, gdn_A_log, gdn_dt_bias, gdn_norm, m_ibias, m_fbias, m_norm,
               hgrn_lb, hgrn_norm, w_br, w_out, ln_ffn, w_up, ffn_conv_w, ffn_conv_b, w_down, ln_final)
    B = x_prompt.shape[0]
    dt = x_prompt.dtype
    H, Dh = N_HEADS, HEAD_DIM
    z_S = jnp.zeros((DEPTH, B, H, Dh, Dh), dt)
    z_gconv = jnp.zeros((DEPTH, B, CONV_W - 1, 3 * MIX_W), dt)
    z_n = jnp.zeros((DEPTH, B, H, Dh), dt)
    z_m = jnp.zeros((DEPTH, B, H), dt)
    z_fconv = jnp.zeros((DEPTH, B, FFN_CONV_W - 1, 2 * D_FF), dt)
    y_prompt, (p_gdn_S, p_gdn_conv, p_mlstm_C, p_mlstm_n, p_mlstm_m, p_hgrn_S, p_ffn_conv) = _trunk(
        x_prompt, z_S, z_gconv, z_S, z_n, z_m, z_S, z_fconv, *weights)
    y_sample, (s_gdn_S, s_gdn_conv, s_mlstm_C, s_mlstm_n, s_mlstm_m, s_hgrn_S, s_ffn_conv) = _trunk(
        x_sample, state_gdn_S, state_gdn_conv, state_mlstm_C, state_mlstm_n, state_mlstm_m,
        state_hgrn_S, state_ffn_conv, *weights)
    return (y_prompt, y_sample,
            p_gdn_S, p_gdn_conv, p_mlstm_C, p_mlstm_n, p_mlstm_m, p_hgrn_S, p_ffn_conv,
            s_gdn_S, s_gdn_conv, s_mlstm_C, s_mlstm_n, s_mlstm_m, s_hgrn_S, s_ffn_conv)
```

```python
from contextlib import ExitStack
import math
import numpy as np
import concourse.bass as bass
import concourse.mybir as mybir
from concourse.bass_utils import run_bass_kernel_spmd

F32 = mybir.dt.float32
BF16 = mybir.dt.bfloat16
ALU = mybir.AluOpType
AF = mybir.ActivationFunctionType

PE, DVE, ACT, POOL, SP = 'tensor', 'vector', 'scalar', 'gpsimd', 'sync'
ENGS = (PE, DVE, ACT, POOL, SP)


HOP = 0.35


class DmaGroup:
    def __init__(self, sem):
        self.sem = sem
        self.cnt = 0


class Tile:
    def __init__(self, name, t, dg=None):
        self.name = name
        self.t = t
        self.w = None
        self.rs = {}
        self.dg = dg
        self.tw = 0.0
        self.tr = 0.0

    def __getitem__(self, idx):
        return self.t[idx]


class Sched:
    def __init__(self, nc, ctx):
        self.nc = nc
        self.ctx = ctx
        self.seq = {e: 0 for e in ENGS}
        self.waited = {e: {} for e in ENGS}
        self.sems = {e: ctx.enter_context(nc.semaphore('clk_' + e)) for e in ENGS}
        self.groups = []
        self.free_groups = []
        self.capture = None
        self.efree = {e: 0.0 for e in ENGS}
        self.scope_tiles = {}
        self.uid = 0

    def sb(self, name, shape, dtype=F32, ctx=None):
        self.uid += 1
        t = (ctx or self.ctx).enter_context(self.nc.sbuf_tensor('%s_%d' % (name, self.uid), list(shape), dtype))
        tl = Tile(name, t)
        if ctx is not None:
            self.scope_tiles.setdefault(id(ctx), []).append(tl)
        return tl

    def release(self, ctx):
        for tl in self.scope_tiles.pop(id(ctx), []):
            if tl.dg is not None:
                self.free_groups.append(tl.dg)
                tl.dg = None

    def ps(self, name, shape, dtype=F32):
        t = self.ctx.enter_context(self.nc.psum_tensor(name, list(shape), dtype))
        return Tile(name, t)

    def group(self, name):
        if self.free_groups:
            return self.free_groups.pop()
        self.uid += 1
        g = DmaGroup(self.ctx.enter_context(self.nc.semaphore('dg_%s_%d' % (name, self.uid))))
        self.groups.append(g)
        return g

    def _need(self, eng, dep, waits, kind):
        if dep is None:
            return
        if dep[0] == 'e':
            _, e2, s2 = dep
            if e2 == eng and kind != 'raw':
                return
            if self.waited[eng].get(e2, 0) >= s2:
                return
            self.waited[eng][e2] = s2
            waits.append((self.sems[e2], s2))
        else:
            dg = dep[1]
            c = dg.cnt
            if self.waited[eng].get(id(dg), 0) >= c:
                return
            self.waited[eng][id(dg)] = c
            waits.append((dg.sem, c))

    def _deps(self, eng, R, W, dma=False):
        waits = []
        for t in R:
            self._need(eng, t.w, waits, 'raw')
        for t in W:
            if not (dma and t.w is not None and t.w[0] == 'd'):
                self._need(eng, t.w, waits, 'waw')
            for r in t.rs.values():
                self._need(eng, r, waits, 'war')
        return waits

    ECOST = {PE: 0.12, DVE: 0.45, ACT: 0.45, POOL: 0.6, SP: 0.1}

    def est_start(self, item):
        kind, eng, R, W = item[0], item[1], item[3], item[4]
        t = self.efree[eng]
        for tl in R:
            t = max(t, tl.tw + HOP)
        for tl in W:
            t = max(t, tl.tw + HOP, tl.tr + HOP)
        return t

    def _est_update(self, kind, eng, R, W, cost):
        t0 = self.est_start((kind, eng, None, R, W))
        if kind == 'dma':
            self.efree[eng] = t0 + 0.1
            t1 = t0 + 2.5
        else:
            t1 = t0 + (cost if cost is not None else self.ECOST[eng])
            self.efree[eng] = t1
        for tl in W:
            tl.tw = t1
            tl.tr = 0.0
        for tl in R:
            tl.tr = max(tl.tr, t1)

    def run_window(self, q, W_):
        q = list(q)
        while q:
            best = None
            seen_r, seen_w = set(), set()
            for pos, it in enumerate(q[:W_]):
                r_ = set(id(t_) for t_ in it[3])
                w_ = set(id(t_) for t_ in it[4])
                if not (w_ & (seen_r | seen_w)) and not (r_ & seen_w):
                    es = self.est_start(it) + 0.02 * pos
                    if best is None or es < best[0]:
                        best = (es, pos)
                seen_r |= r_
                seen_w |= w_
            self.emit(q.pop(best[1]))

    def emit(self, item):
        cap, self.capture = self.capture, None
        if item[0] == 'op':
            self.op(item[1], item[2], item[3], item[4], cost=item[5])
        else:
            self.dma(item[1], item[2][0], item[2][1], item[3], item[4], **item[5])
        self.capture = cap

    def op(self, eng, fn, R=(), W=(), cost=None):
        if self.capture is not None:
            self.capture.append(('op', eng, fn, tuple(R), tuple(W), cost))
            return
        self._est_update('op', eng, R, W, cost)
        waits = self._deps(eng, R, W)
        self.seq[eng] += 1
        h = getattr(self.nc, eng)
        for (sm, v) in waits:
            h.wait_ge(sm, v)
        fn(h).then_inc(self.sems[eng], 1)
        me = ('e', eng, self.seq[eng])
        for t in W:
            t.w = me
            t.rs = {}
        for t in R:
            if t not in W:
                t.rs[eng] = me

    def dma(self, eng, out, in_, R=(), W=(), **kw):
        if self.capture is not None:
            self.capture.append(('dma', eng, (out, in_), tuple(R), tuple(W), kw))
            return
        self._est_update('dma', eng, R, W, None)
        waits = self._deps(eng, R, W, dma=True)
        tl = (list(W) + list(R))[0]
        if tl.dg is None:
            tl.dg = self.group(tl.name)
        dg = tl.dg
        dg.cnt += 16
        h = getattr(self.nc, eng)
        for (sm, v) in waits:
            h.wait_ge(sm, v)
        h.dma_start(out=out, in_=in_, **kw).then_inc(dg.sem, 16)
        me = ('d', dg, dg.cnt)
        for t in W:
            t.w = me
            t.rs = {}
        for t in R:
            t.rs[id(dg)] = me

    def barrier(self):
        for e in ENGS:
            h = getattr(self.nc, e)
            for e2 in ENGS:
                if e2 != e and self.seq[e2] > self.waited[e].get(e2, 0):
                    self.waited[e][e2] = self.seq[e2]
                    h.wait_ge(self.sems[e2], self.seq[e2])
            for g in self.groups:
                if g.cnt > self.waited[e].get(id(g), 0):
                    self.waited[e][id(g)] = g.cnt
                    h.wait_ge(g.sem, g.cnt)


D = 1024
KC = 8
TP = 2048
NSQ = 16
T = TP + NSQ * 4
DEPTH = 2
NIN = 9232
DFF = 2816
NFC = 44
EPS = 1e-6
O_GQKV, O_GZ, O_GB, O_GA = 0, 1536, 2048, 2052
O_MQ, O_MK, O_MV, O_MO, O_MI, O_MF = 2056, 2568, 3080, 3592, 4104, 4108
O_HQ, O_HF, O_HI, O_HG, O_BR = 4112, 4624, 5136, 5648, 6160


class Grp:
    def __init__(self, tok0, ntok, c, nch, tt, off, samp, idx, s0=0):
        self.tok0, self.ntok, self.c, self.nch, self.tt, self.off, self.samp, self.idx = tok0, ntok, c, nch, tt, off, samp, idx
        self.s0 = s0


NWQ = 4
NPG = 4
GROUPS = [Grp(512 * i, 512, 64, 8, i, 0, False, i) for i in range(NPG)] + [Grp(TP + 32 * j, 32, 4, 8, 4, 32 * j, True, NPG + j, 8 * j) for j in range(2)]
TT_N = [512, 512, 512, 512, 64]

PP = {}
_o = 0
for _n, _w in (('lnmix', 16), ('lnffn', 16), ('lnfin', 8), ('gconvw', 96), ('gnorm', 8), ('mnorm', 8), ('hnorm', 8),
               ('hlb', 8), ('scpar', 4), ('fconvw', 2 * NFC * 3), ('fconvb', 2 * NFC), ('min', 128), ('nin', 128)):
    PP[_n] = _o
    _o += _w
NPP = _o
CC = {}
_o = 0
for _n, _w in (('ident', 128), ('sel', 512), ('sel3', 512),
               ('mTc4', 64), ('mTs4', 64), ('mS4', 64), ('id4', 64), ('rs256', 512), ('rs64', 64),
               ('mTc64', 512), ('mTs64', 512), ('mS64', 512), ('id64', 512)):
    CC[_n] = _o
    _o += _w
NCC = _o
NCC_A = CC['mTc64']


def build_consts():
    c = np.zeros((128, NCC), np.float32)
    c[:, CC['ident']:CC['ident'] + 128] = np.eye(128)
    sel = np.zeros((128, 4, 128), np.float32)
    for q in range(4):
        for j in range(4):
            sel[32 * q + j, j, :] = 1.0
    c[:, CC['sel']:CC['sel'] + 512] = sel.reshape(128, 512)
    sel3 = np.zeros((128, 4, 128), np.float32)
    for j in range(4):
        sel3[96 + j, j, :] = 1.0
    c[:, CC['sel3']:CC['sel3'] + 512] = sel3.reshape(128, 512)
    for cs, nch, sfx in ((64, 8, '64'), (4, 16, '4')):
        s = np.arange(cs)[:, None]
        t = np.arange(cs)[None, :]
        for nm, m in (('mTc', t >= s), ('mTs', t > s), ('mS', t < s), ('id', t == s)):
            c[:cs, CC[nm + sfx]:CC[nm + sfx] + nch * cs] = np.tile(m.astype(np.float32), (1, nch))
    c[:, CC['rs256']:CC['rs256'] + 512] = (np.arange(512) % 64 != 0).astype(np.float32)[None, :]
    c[:, CC['rs64']:CC['rs64'] + 64] = (np.arange(64) % 4 != 0).astype(np.float32)[None, :]
    return c


CFG = dict(layers=DEPTH, mixers=(0, 1, 2), heads=4, merge=True, ffn=True, groups=None)


def build_nc():
    nc = bass.Bass("TRN2", target_bir_lowering=False)
    GRPS = GROUPS if CFG['groups'] is None else [GROUPS[i] for i in CFG['groups']]

    def din(name, shape):
        return nc.dram_tensor(name, list(shape), F32, kind="ExternalInput").ap()

    def dout(name, shape):
        return nc.dram_tensor(name, list(shape), F32, kind="ExternalOutput").ap()

    xT_d = din("xT", [D, T])
    w_in_d = din("w_in", [DEPTH, D, NIN])
    w_br_d = din("w_br", [DEPTH, 3, 512, D])
    w_out_d = din("w_out", [DEPTH, D, D])
    w_up_d = din("w_up", [DEPTH, D, 2 * DFF])
    w_dn_d = din("w_down", [DEPTH, DFF, D])
    pp_d = din("pp", [128, NPP])
    cc_d = din("cc", [128, NCC])
    sS_d = {m: din("s%s_S" % m, [DEPTH, NSQ, 4, 128, 128]) for m in 'gmh'}
    gcs_d = din("gcs", [DEPTH, NSQ * 3, 1536])
    fcs_d = din("fcs", [DEPTH, NSQ * 2, 2 * DFF])
    yT_d = dout("yT", [D, T])
    pS_o = {m: dout("p%s_S" % m, [DEPTH, 4, 128, 128]) for m in 'gmh'}
    sS_o = {m: dout("o%s_S" % m, [DEPTH, NSQ, 4, 128, 128]) for m in 'gmh'}
    pgc_o = dout("pgc", [DEPTH, 3, 1536])
    sgc_o = dout("sgc", [DEPTH, NSQ * 3, 1536])
    pfc_o = dout("pfc", [DEPTH, 2, 2 * DFF])
    sfc_o = dout("sfc", [DEPTH, NSQ * 2, 2 * DFF])
    nst_o = dout("nst", [128, 8 + 128])
    mst_o = dout("mst", [1, 8 + 128])

    with ExitStack() as ctx:
        S = Sched(nc, ctx)

        def fsz(ap):
            n_ = 1
            for d_ in ap.shape[1:]:
                n_ *= d_
            return n_

        def ecost(eng, ap):
            k = fsz(ap)
            if eng == DVE:
                return 0.08 + k * 1.05e-3
            if eng == ACT:
                return 0.2 + k * 0.75e-3
            if eng == POOL:
                return 0.3 + k * 1.5e-3
            return 0.05 + k * 0.45e-3

        def vtt(out, in0, in1, op, R, W, eng=DVE):
            S.op(eng, lambda e: e.tensor_tensor(out=out, in0=in0, in1=in1, op=op), R, W, cost=ecost(eng, out))

        def vts(out, in0, s1, s2, op0, op1, R, W, eng=DVE):
            if op1 is None:
                S.op(eng, lambda e: e.tensor_scalar(out=out, in0=in0, scalar1=s1, scalar2=None, op0=op0), R, W, cost=ecost(eng, out))
            else:
                S.op(eng, lambda e: e.tensor_scalar(out=out, in0=in0, scalar1=s1, scalar2=s2, op0=op0, op1=op1), R, W, cost=ecost(eng, out))

        def vstt(out, in0, sc, in1, op0, op1, R, W, eng=DVE):
            S.op(eng, lambda e: e.scalar_tensor_tensor(out=out, in0=in0, scalar=sc, in1=in1, op0=op0, op1=op1), R, W, cost=ecost(eng, out))

        def vcopy(out, in_, R, W, eng=DVE):
            if eng == ACT:
                S.op(ACT, lambda e: e.copy(out=out, in_=in_), R, W, cost=ecost(ACT, out))
            else:
                S.op(eng, lambda e: e.tensor_copy(out=out, in_=in_), R, W, cost=ecost(eng, out))

        def vrecip(out, in_, R, W):
            S.op(DVE, lambda e: e.reciprocal(out=out, in_=in_), R, W)

        def vset(ap, val, W, eng=DVE):
            S.op(eng, lambda e: e.memset(ap, val), (), W)

        def act(out, in_, func, R, W, bias=None, scale=None):
            kw = {}
            if bias is not None:
                kw['bias'] = bias
            if scale is not None:
                kw['scale'] = scale
            S.op(ACT, lambda e: e.activation(out=out, in_=in_, func=func, **kw), R, W, cost=ecost(ACT, out))

        def mm(out, lhsT, rhs, start, stop, R, W):
            S.op(PE, lambda e: e.matmul(out, lhsT=lhsT, rhs=rhs, start=start, stop=stop), R, W, cost=ecost(PE, rhs))

        def tr(out, in_, idn, R, W):
            S.op(PE, lambda e: e.transpose(out, in_, idn), R, W, cost=0.12)

        def scan(out, d0, d1, init, op0, op1, R, W):
            S.op(DVE, lambda e: e.tensor_tensor_scan(out=out, data0=d0, data1=d1, initial=init, op0=op0, op1=op1), R, W, cost=0.08 + fsz(out) * 2.1e-3)

        ACT_COPY = ACT if CFG.get('actcopy', 1) else DVE
        ACT_E = ACT if CFG.get('acte', 1) else DVE
        POOL_E = POOL if CFG.get('pool', 1) else DVE

        def a_sigmoid(out, in_, R, Wt):
            act(out, in_, AF.Exp, R, [Wt], scale=-1.0)
            act(out, out, AF.Ln, [Wt], [Wt], bias=1.0)
            act(out, out, AF.Exp, [Wt], [Wt], scale=-1.0)

        def a_rsqrt(out, in_, scale, R, Wt):
            act(out, in_, AF.Ln, list(R) + [epst], [Wt], bias=epst[0:out.shape[0], 0:1], scale=scale)
            act(out, out, AF.Exp, [Wt], [Wt], scale=-0.5)

        def a_recip(out, in_, R, Wt):
            act(out, in_, AF.Ln, R, [Wt])
            act(out, out, AF.Exp, [Wt], [Wt], scale=-1.0)
        xT = [None] * 5
        xscr = nc.dram_tensor("xscr", [128, KC * T], F32, kind="Internal").ap()
        hT = [S.sb("hT%d" % i, [128, KC, TT_N[i]], BF16) for i in range(5)]
        ob = S.sb("ob", [128, 4, T], BF16)
        mscr = nc.dram_tensor("mscr", [128, 8 * T], BF16, kind="Internal").ap()
        ppt = S.sb("pp", [128, NPP])
        cct = S.sb("cc", [128, NCC_A])
        ccb = S.sb("ccb", [64, NCC - NCC_A], BF16)
        ones_bf = S.sb("ones", [128, 128], BF16)
        epst = S.sb("eps", [128, 1])
        psb = [S.ps("ps%d" % i, [128, 512]) for i in range(8)]
        psi = [0]

        def PS():
            psi[0] = (psi[0] + 1) % 8
            return psb[psi[0]]

        psa = [0]

        def PSA():
            psa[0] = (psa[0] + 1) % 3
            return psb[psa[0]]

        nst = S.sb("nst", [128, 136])
        mst = S.sb("mst", [1, 136])
        wq = [S.sb("wq%d" % i, [128, KC, 128], BF16) for i in range(NWQ)]
        wqi = [0]

        def ident(n):
            return cct[0:n, CC['ident']:CC['ident'] + n]

        def ppc(name, idx):
            o = PP[name] + idx
            return ppt[:, o:o + 1]

        S.dma(SP, ppt[:], pp_d, W=[ppt])
        S.dma(SP, cct[:], cc_d[:, 0:NCC_A], W=[cct])
        S.dma(POOL, ccb[:], cc_d[0:64, NCC_A:NCC], W=[ccb])
        vset(ones_bf[:], 1.0, [ones_bf])
        vset(epst[:], EPS, [epst])
        xctx = [None]

        def x_alloc():
            xctx[0] = ExitStack()
            for i in range(5):
                xT[i] = S.sb("xT%d" % i, [128, KC, TT_N[i]], F32, xctx[0])

        def x_free():
            S.barrier()
            S.release(xctx[0])
            xctx[0].close()

        def xs_view(i):
            o = sum(TT_N[:i])
            return xscr.rearrange("p (kc t) -> p kc t", kc=KC)[:, :, o:o + TT_N[i]]

        def loadw(src, ncols=128):
            wqi[0] = (wqi[0] + 1) % NWQ
            w = wq[wqi[0]]
            S.dma(POOL, w[:, :, 0:ncols], src.rearrange("(kc p) j -> p kc j", p=128), W=[w])
            return w

        def proj(w, g, out, ncols=128):
            pst = out
            co = 0
            if isinstance(w, tuple):
                w, co = w
            for kc in range(KC):
                mm(pst[0:ncols, 0:g.ntok], w[:, kc, co:co + ncols], hT[g.tt][:, kc, g.off:g.off + g.ntok],
                   kc == 0, kc == KC - 1, [w, hT[g.tt]], [pst])

        def rmsnorm(dst_tiles, gname, gidx0, dst_dtype_f32=False, store=None):
            with ExitStack() as c2:
                sq = [S.sb("sq%d" % i, [128, 512], BF16, c2) for i in range(2)]
                rs = S.sb("rs", [128, 512], F32, c2)
                yo = [S.sb("yo%d" % i, [128, 512], F32, c2) for i in range(2)] if store is not None else None
                rq = []
                S.capture = rq
                for tt in range(5):
                    n = TT_N[tt]
                    p = PS()
                    for kc in range(KC):
                        s_ = sq[kc % 2]
                        act(s_[:, 0:n], xT[tt][:, kc, :], AF.Square, [xT[tt]], [s_])
                        mm(p[:, 0:n], ones_bf[:], s_[:, 0:n], kc == 0, kc == KC - 1, [ones_bf, s_], [p])
                    a_rsqrt(rs[:, 0:n], p[:, 0:n], 1.0 / D, [p], rs)
                    for kc in range(KC):
                        if store is None:
                            vstt(dst_tiles[tt][:, kc, :], xT[tt][:, kc, :], ppc(gname, gidx0 + kc), rs[:, 0:n],
                                 ALU.mult, ALU.mult, [xT[tt], ppt, rs], [dst_tiles[tt]])
                        else:
                            y_ = yo[kc % 2]
                            vstt(y_[:, 0:n], xT[tt][:, kc, :], ppc(gname, gidx0 + kc), rs[:, 0:n],
                                 ALU.mult, ALU.mult, [xT[tt], ppt, rs], [y_])
                            o = sum(TT_N[:tt])
                            S.dma(SP, store[kc * 128:(kc + 1) * 128, o:o + n], y_[:, 0:n], R=[y_])
                S.capture = None
                S.run_window(rq, 32)
                S.barrier()
                S.release(c2)

        def cmask(name, g):
            sfx = '64' if g.c == 64 else '4'
            o = CC[name + sfx]
            if g.c == 64:
                return ccb[0:64, o - NCC_A:o - NCC_A + g.nch * g.c]
            return cct[0:g.c, o:o + g.nch * g.c]

        def mtile(g):
            return ccb if g.c == 64 else cct

        marks = []
        _NC_CACHE['marks'] = marks

        def mark(label):
            marks.append((label, dict(S.seq)))

        x_alloc()
        for i in range(5):
            o = sum(TT_N[:i])
            S.dma(SP, xT[i][:], xT_d[:, o:o + TT_N[i]].rearrange("(kc p) t -> p kc t", p=128), W=[xT[i]])
        for l in range(CFG['layers']):
            mark('L%d rmsnorm1' % l)
            if l > 0:
                for i in range(5):
                    S.dma(SP, xs_view(i), xT[i][:], R=[xT[i]])
            rmsnorm(hT, 'lnmix', l * 8)
            x_free()

            for bi in range(3):
                if bi not in CFG['mixers']:
                    continue
                mark('L%d mixer%d' % (l, bi))
                with ExitStack() as cm:
                    NT = 512
                    def TF(name, shape=None, dt=F32):
                        return S.sb(name, shape or [128, NT], dt, cm)
                    scw = TF("scw", [128, KC, 128], BF16)
                    vset(scw[:], 0.0, [scw])
                    for q, o in enumerate((O_GB, O_GA, O_MI, O_MF)):
                        S.dma(POOL, scw[:, :, 32 * q:32 * q + 4],
                              w_in_d[l][:, o:o + 4].rearrange("(kc p) j -> p kc j", p=128), W=[scw])

                    hwp = [TF("hwp%d" % i, [128, KC, 256], BF16) for i in range(4)]

                    def make_ctx(cix):
                        gcis = [TF("gci%d" % i, [48, 128]) for i in range(2)]
                        csts = [TF("cst%d" % i, [48, 128]) for i in range(2)]
                        cio = [0]
                        psa = [0]

                        def PSA():
                            psa[0] = (psa[0] + 1) % 2
                            return psb[2 * cix + psa[0]]

                        psr = [0]

                        def PS():
                            psr[0] = (psr[0] + 1) % 2
                            return psb[4 + 2 * cix + psr[0]]

                        f = [TF("f%d" % i) for i in range(10)]
                        b = [TF("b%d" % i, None, BF16) for i in range(8)]
                        tk = [TF("tk%d" % i, [64, 8 * 130], BF16) for i in range(3)]
                        sct = f[9]
                        uvt = TF("uvt", [64, 1024], BF16)
                        pre = TF("pre", [128, 3 + NT])
                        pre2 = TF("pre2", [128, 3 + NT])
                        cso2 = TF("cso2", [128, 48])
                        hist = [TF("hist%d" % i, [128, 3]) for i in range(3)]
                        cso = TF("cso", [128, 48])
                        Sst = [TF("Sst%d" % i, [128, 130]) for i in range(8)]
                        Sbf = [TF("Sbf%d" % i, [128, 130], BF16) for i in range(8)]
                        nrep = [TF("nrep%d" % i, [128, 128], BF16) for i in range(8)]
                        ubig = TF("ubig", [64, 8 * 128], BF16)
                        sm = [TF("sm%d" % i, [128, 16]) for i in range(6)]
                        mcar = TF("mcar", [128, 1])
                        cols = TF("cols", [64, 16])
                        prm = TF("prm", [128, 4])

                        act(prm[:, 0:1], ppc('scpar', l * 2 + 1), AF.Exp, [ppt], [prm])
                        vts(prm[:, 0:1], prm[:, 0:1], -1.0, None, ALU.mult, None, [prm], [prm])
                        vts(prm[:, 1:2], ppc('scpar', l * 2), -1.0, None, ALU.mult, None, [ppt], [prm])
                        vset(sct[:], 0.0, [sct])

                        def load_head_w(cols_list):
                            return [(hwp[i], cix * 128) for i in range(4)]

                        def scalars(g):
                            n = g.ntok
                            p = PS()
                            proj(scw, g, p)
                            a_sigmoid(sct[0:4, 0:n], p[0:4, 0:n], [p], sct)
                            act(sct[32:36, 0:n], p[32:36, 0:n], AF.Exp, [p, ppt], [sct], bias=ppt[32:36, PP['scpar'] + l * 2:PP['scpar'] + l * 2 + 1])
                            act(sct[32:36, 0:n], sct[32:36, 0:n], AF.Ln, [sct], [sct], bias=1.0)
                            vts(sct[32:36, 0:n], sct[32:36, 0:n], prm[32:36, 0:1], None, ALU.mult, None, [sct, prm], [sct])
                            act(sct[64:68, 0:n], p[64:68, 0:n], AF.Identity, [p, ppt], [sct], bias=ppt[64:68, PP['scpar'] + l * 2:PP['scpar'] + l * 2 + 1])
                            act(sct[96:100, 0:n], p[96:100, 0:n], AF.Exp, [p, prm], [sct], bias=prm[96:100, 1:2], scale=-1.0)
                            act(sct[96:100, 0:n], sct[96:100, 0:n], AF.Ln, [sct], [sct], bias=1.0)
                            vts(sct[96:100, 0:n], sct[96:100, 0:n], -1.0, None, ALU.mult, None, [sct], [sct])

                        def bcast(q, h, g, dst):
                            n = g.ntok
                            p = PS()
                            if q < 3:
                                so = CC['sel'] + h * 128
                                mm(p[:, 0:n], cct[32 * q:32 * q + 4, so:so + 128], sct[32 * q:32 * q + 4, 0:n], True, True, [cct, sct], [p])
                            else:
                                so = CC['sel3'] + h * 128
                                mm(p[:, 0:n], cct[64:100, so:so + 128], sct[64:100, 0:n], True, True, [cct, sct], [p])
                            vcopy(dst[:, 0:n], p[:, 0:n], [p], [dst], eng=ACT_COPY)

                        def to_tok(src, g, dst, width=128, stride=None, bank=None):
                            stride = stride or width
                            c, nch = g.c, g.nch
                            for b0 in range(0, nch, 4):
                                p = bank if bank is not None else PS()
                                for j in range(4):
                                    ci = b0 + j
                                    tr(p[0:c, j * 128:(j + 1) * 128], src[:, ci * c:(ci + 1) * c], ident(128), [src, cct], [p])
                                dv = dst[0:c, b0 * stride:(b0 + 4) * stride].rearrange("p (a w) -> p a w", a=4)[:, :, 0:128]
                                vcopy(dv, p[0:c, 0:512].rearrange("p (a w) -> p a w", a=4), [p], [dst], eng=ACT_E)

                        def post(osrc, osrc_t, gate_p, gate_f, normname, h, g, bi):
                            n = g.ntok
                            act(b[7][:, 0:n], osrc, AF.Square, [osrc_t], [b[7]])
                            p = PS()
                            mm(p[:, 0:n], ones_bf[:], b[7][:, 0:n], True, True, [ones_bf, b[7]], [p])
                            a_rsqrt(f[8][:, 0:n], p[:, 0:n], 1.0 / 128, [p], f[8])
                            vstt(f[8][:, 0:n], osrc, ppc(normname, l * 4 + h), f[8][:, 0:n], ALU.mult, ALU.mult, [osrc_t, ppt, f[8]], [f[8]])
                            if gate_f == AF.Silu:
                                vtt(f[8][:, 0:n], f[8][:, 0:n], gate_p[:, 0:n], ALU.mult, [f[8], gate_p], [f[8]])
                            a_sigmoid(f[7][:, 0:n], gate_p[:, 0:n], [gate_p], f[7])
                            vtt(ob[:, h, g.tok0:g.tok0 + n], f[8][:, 0:n], f[7][:, 0:n], ALU.mult, [f[8], f[7]], [ob])

                        def lastbc(t_, g):
                            c, nch = g.c, g.nch
                            v = t_[:, 0:g.ntok].rearrange("p (a c) -> p a c", c=c)
                            return v[:, :, c - 1:c].broadcast_to([128, nch, c]), v

                        def state_io(mname, h, g, ci, k, width, when):
                            if when == 'load':
                                S.dma(SP, Sst[k][:, 0:128], sS_d[mname][l, g.s0 + ci, h], W=[Sst[k]])
                            elif g.samp:
                                S.dma(SP, sS_o[mname][l, g.s0 + ci, h], Sst[k][:, 0:128], R=[Sst[k]])
                            else:
                                S.dma(SP, pS_o[mname][l, h], Sst[k][:, 0:128], R=[Sst[k]])


                        def hgrn_head(h):
                            ws = load_head_w([O_HQ + h * 128, O_HF + h * 128, O_HI + h * 128, O_HG + h * 128])
                            if l == 0:
                                vset(prm[:, 2:3], 0.0, [prm])
                            else:
                                vtt(prm[:, 2:3], ppc('hlb', 4 + h), ppc('hlb', h), ALU.subtract, [ppt], [prm])
                                yield
                                a_sigmoid(prm[:, 2:3], prm[:, 2:3], [prm], prm)
                                yield
                            vts(prm[:, 3:4], prm[:, 2:3], -1.0, 1.0, ALU.mult, ALU.add, [prm], [prm])
                            vset(Sst[0][:], 0.0, [Sst[0]])
                            vset(Sbf[0][:], 0.0, [Sbf[0]])
                            for g in GRPS:
                                n, c, nch = g.ntok, g.c, g.nch
                                rsm = cct[:, CC['rs256']:CC['rs256'] + n] if not g.samp else cct[:, CC['rs64']:CC['rs64'] + n]
                                qs, fg, G, kk, t1, kw, iT = f[0], f[1], f[2], f[3], f[4], f[5], f[6]
                                qe, ke = b[0], b[1]
                                kw_tok, i_tok = tk[0], tk[1]
                                p = PS(); proj(ws[0], g, p)
                                yield
                                a_sigmoid(qs[:, 0:n], p[:, 0:n], [p], qs)
                                yield
                                vtt(qs[:, 0:n], qs[:, 0:n], p[:, 0:n], ALU.mult, [qs, p], [qs])
                                yield
                                p = PS(); proj(ws[1], g, p)
                                yield
                                a_sigmoid(fg[:, 0:n], p[:, 0:n], [p], fg)
                                yield
                                vts(fg[:, 0:n], fg[:, 0:n], prm[:, 3:4], prm[:, 2:3], ALU.mult, ALU.add, [fg, prm], [fg])
                                yield
                                act(t1[:, 0:n], fg[:, 0:n], AF.Ln, [fg], [t1])
                                yield
                                vts(kk[:, 0:n], fg[:, 0:n], -1.0, 1.0, ALU.mult, ALU.add, [fg], [kk])
                                yield
                                scan(G[:, 0:n], rsm, t1[:, 0:n], 0.0, ALU.mult, ALU.add, [cct, t1], [G])
                                yield
                                act(t1[:, 0:n], G[:, 0:n], AF.Exp, [G], [t1])
                                yield
                                vtt(qe[:, 0:n], qs[:, 0:n], t1[:, 0:n], ALU.mult, [qs, t1], [qe])
                                yield
                                act(t1[:, 0:n], G[:, 0:n], AF.Exp, [G], [t1], scale=-1.0)
                                yield
                                vtt(ke[:, 0:n], kk[:, 0:n], t1[:, 0:n], ALU.mult, [kk, t1], [ke])
                                yield
                                lb_, Gv = lastbc(G, g)
                                vtt(t1[:, 0:n].rearrange("p (a c) -> p a c", c=c), lb_, Gv, ALU.subtract, [G], [t1])
                                yield
                                act(t1[:, 0:n], t1[:, 0:n], AF.Exp, [t1], [t1])
                                yield
                                vtt(kw[:, 0:n], kk[:, 0:n], t1[:, 0:n], ALU.mult, [kk, t1], [kw])
                                yield
                                act(t1[:, 0:n], G[:, 0:n], AF.Exp, [G], [t1])
                                yield
                                to_tok(kw, g, kw_tok)
                                yield
                                p = PS(); proj(ws[2], g, p)
                                yield
                                vcopy(iT[:, 0:n], p[:, 0:n], [p], [iT], eng=ACT_E)
                                yield
                                to_tok(iT, g, i_tok)
                                yield
                                pg = PSA(); proj(ws[3], g, pg)
                                vcopy(f[9][:, 0:n], pg[:, 0:n], [pg], [f[9]], eng=ACT_E)
                                pg = f[9]
                                yield
                                pp_ = PS()
                                for ci in range(nch):
                                    mm(pp_[0:c, ci * c:(ci + 1) * c], ke[:, ci * c:(ci + 1) * c], qe[:, ci * c:(ci + 1) * c], True, True, [ke, qe], [pp_])
                                PTs = b[2]
                                vtt(PTs[0:c, 0:n], pp_[0:c, 0:n], cmask('mTc', g), ALU.mult, [pp_, mtile(g)], [PTs])
                                yield
                                po = PSA()
                                if g.samp:
                                    for ci in range(nch):
                                        state_io('h', h, g, ci, ci, 128, 'load')
                                    yield
                                    for ci in range(nch):
                                        vcopy(Sbf[ci][:, 0:128], Sst[ci][:, 0:128], [Sst[ci]], [Sbf[ci]], eng=ACT_COPY)
                                    yield
                                    for ci in range(nch):
                                        cs = slice(ci * c, (ci + 1) * c)
                                        mm(po[:, cs], Sbf[ci][:, 0:128], qe[:, cs], True, False, [Sbf[ci], qe], [po])
                                        mm(po[:, cs], i_tok[0:c, ci * 128:(ci + 1) * 128], PTs[0:c, cs], False, True, [i_tok, PTs], [po])
                                    for b0 in range(0, nch, 4):
                                        pS_ = PS()
                                        for j in range(4):
                                            ci = b0 + j
                                            mm(pS_[:, j * 128:(j + 1) * 128], kw_tok[0:c, ci * 128:(ci + 1) * 128], i_tok[0:c, ci * 128:(ci + 1) * 128], True, True, [kw_tok, i_tok], [pS_])
                                        for j in range(4):
                                            ci = b0 + j
                                            ce = (ci + 1) * c - 1
                                            vstt(Sst[ci][:, 0:128], Sst[ci][:, 0:128], t1[:, ce:ce + 1], pS_[:, j * 128:(j + 1) * 128], ALU.mult, ALU.add, [Sst[ci], t1, pS_], [Sst[ci]])
                                        yield
                                    for ci in range(nch):
                                        state_io('h', h, g, ci, ci, 128, 'store')
                                    yield
                                for ci in (range(nch) if not g.samp else []):
                                    k = 0
                                    cs = slice(ci * c, (ci + 1) * c)
                                    mm(po[:, cs], Sbf[k][:, 0:128], qe[:, cs], True, False, [Sbf[k], qe], [po])
                                    mm(po[:, cs], i_tok[0:c, ci * 128:(ci + 1) * 128], PTs[0:c, cs], False, True, [i_tok, PTs], [po])
                                    pS_ = PS()
                                    mm(pS_[:, 0:128], kw_tok[0:c, ci * 128:(ci + 1) * 128], i_tok[0:c, ci * 128:(ci + 1) * 128], True, True, [kw_tok, i_tok], [pS_])
                                    ce = (ci + 1) * c - 1
                                    vstt(Sst[k][:, 0:128], Sst[k][:, 0:128], t1[:, ce:ce + 1], pS_[:, 0:128], ALU.mult, ALU.add, [Sst[k], t1, pS_], [Sst[k]])
                                    yield
                                    if g.samp or (g.idx == NPG - 1 and ci == nch - 1):
                                        state_io('h', h, g, ci, k, 128, 'store')
                                        yield
                                    if not g.samp:
                                        vcopy(Sbf[k][:, 0:128], Sst[k][:, 0:128], [Sst[k]], [Sbf[k]], eng=ACT_COPY)
                                        yield
                                post(po[:, 0:n], po, pg, AF.Silu, 'hnorm', h, g, 2)
                                yield

                        def mlstm_head(h):
                            ws = load_head_w([O_MQ + h * 128, O_MK + h * 128, O_MV + h * 128, O_MO + h * 128])
                            vset(Sst[0][:], 0.0, [Sst[0]])
                            vset(Sbf[0][:], 0.0, [Sbf[0]])
                            vset(nrep[0][:], 0.0, [nrep[0]])
                            vset(mcar[:], 0.0, [mcar])
                            v_tok, kw_tok = tk[0], tk[1]
                            for g in GRPS:
                                n, c, nch = g.ntok, g.c, g.nch
                                rsm = cct[:, CC['rs256']:CC['rs256'] + n] if not g.samp else cct[:, CC['rs64']:CC['rs64'] + n]
                                ig, lf, Fc, m, t1, t2, vT, t3 = f[0], f[1], f[2], f[3], f[4], f[5], f[6], f[7]
                                scalars(g)
                                yield
                                bcast(2, h, g, ig)
                                yield
                                bcast(3, h, g, lf)
                                yield
                                scan(Fc[:, 0:n], rsm, lf[:, 0:n], 0.0, ALU.mult, ALU.add, [cct, lf], [Fc])
                                yield
                                mp = sm[0]
                                mv = m[:, 0:n].rearrange("p (a c) -> p a c", c=c)
                                if not g.samp:
                                    vcopy(mp[:, 0:1], mcar[:, 0:1], [mcar], [mp])
                                    yield
                                    scan(m[:, 0:n], lf[:, 0:n], ig[:, 0:n], mcar[:, 0:1], ALU.add, ALU.max, [lf, ig, mcar], [m])
                                    yield
                                    vcopy(mp[:, 1:nch], mv[:, 0:nch - 1, c - 1], [m], [mp])
                                    yield
                                    vcopy(mcar[:, 0:1], m[:, n - 1:n], [m], [mcar])
                                    yield
                                else:
                                    mo = PP['min'] + l * 64 + g.s0 * 4 + h
                                    vcopy(mp[:, 0:nch], ppt[:, mo:mo + 4 * nch].rearrange("p (s h) -> p s h", h=4)[:, :, 0], [ppt], [mp])
                                    yield
                                    lfv = lf[:, 0:n].rearrange("p (a c) -> p a c", c=c)
                                    igv = ig[:, 0:n].rearrange("p (a c) -> p a c", c=c)
                                    for t_ in range(4):
                                        prev = mp[:, 0:nch] if t_ == 0 else mv[:, :, t_ - 1]
                                        vtt(sm[1][:, 0:nch], lfv[:, :, t_], prev, ALU.add, [lf, mp, m], [sm[1]])
                                        yield
                                        vtt(mv[:, :, t_], sm[1][:, 0:nch], igv[:, :, t_], ALU.max, [sm[1], ig], [m])
                                        yield
                                Fv = Fc[:, 0:n].rearrange("p (a c) -> p a c", c=c)
                                t1v = t1[:, 0:n].rearrange("p (a c) -> p a c", c=c)
                                inter = t2
                                vtt(t1v, Fv, mp[:, 0:nch].unsqueeze(2).broadcast_to([128, nch, c]), ALU.add, [Fc, mp], [t1])
                                yield
                                vtt(t1[:, 0:n], t1[:, 0:n], m[:, 0:n], ALU.subtract, [t1, m], [t1])
                                yield
                                act(inter[:, 0:n], t1[:, 0:n], AF.Exp, [t1], [inter])
                                yield
                                vtt(t1[:, 0:n], Fc[:, 0:n], m[:, 0:n], ALU.subtract, [Fc, m], [t1])
                                yield
                                act(t1[:, 0:n], t1[:, 0:n], AF.Exp, [t1], [t1])
                                yield
                                q2, qi, k2 = b[0], b[1], b[2]
                                p = PS(); proj(ws[0], g, p)
                                yield
                                vtt(q2[:, 0:n], p[:, 0:n], t1[:, 0:n], ALU.mult, [p, t1], [q2])
                                yield
                                vtt(qi[:, 0:n], p[:, 0:n], inter[:, 0:n], ALU.mult, [p, inter], [qi])
                                yield
                                vtt(t1[:, 0:n], ig[:, 0:n], Fc[:, 0:n], ALU.subtract, [ig, Fc], [t1])
                                yield
                                act(t3[:, 0:n], t1[:, 0:n], AF.Exp, [t1], [t3])
                                yield
                                Fl, _ = lastbc(Fc, g)
                                ml, _ = lastbc(m, g)
                                vtt(t1v, t1v, Fl, ALU.add, [t1, Fc], [t1])
                                yield
                                vtt(t1v, t1v, ml, ALU.subtract, [t1, m], [t1])
                                yield
                                act(t1[:, 0:n], t1[:, 0:n], AF.Exp, [t1], [t1])
                                yield
                                p = PS(); proj(ws[1], g, p)
                                yield
                                vstt(k2[:, 0:n], p[:, 0:n], 128.0 ** -0.5, t3[:, 0:n], ALU.mult, ALU.mult, [p, t3], [k2])
                                yield
                                vstt(t3[:, 0:n], p[:, 0:n], 128.0 ** -0.5, t1[:, 0:n], ALU.mult, ALU.mult, [p, t1], [t3])
                                yield
                                to_tok(t3, g, kw_tok)
                                yield
                                p = PS(); proj(ws[2], g, p)
                                yield
                                vcopy(vT[:, 0:n], p[:, 0:n], [p], [vT], eng=ACT_E)
                                yield
                                vset(v_tok[0:c, :], 1.0, [v_tok])
                                to_tok(vT, g, v_tok, 128, 130)
                                yield
                                pg = PS(); proj(ws[3], g, pg)
                                yield
                                vcopy(f[9][:, 0:n], pg[:, 0:n], [pg], [f[9]], eng=ACT_E)
                                yield
                                pg = f[9]
                                pp_ = PS()
                                for ci in range(nch):
                                    mm(pp_[0:c, ci * c:(ci + 1) * c], k2[:, ci * c:(ci + 1) * c], q2[:, ci * c:(ci + 1) * c], True, True, [k2, q2], [pp_])
                                PTs = b[3]
                                vtt(PTs[0:c, 0:n], pp_[0:c, 0:n], cmask('mTc', g), ALU.mult, [pp_, mtile(g)], [PTs])
                                yield
                                pnum = PSA()
                                pden = PSA()
                                if g.samp:
                                    for ci in range(nch):
                                        state_io('m', h, g, ci, ci, 128, 'load')
                                        no = PP['nin'] + l * 64 + (g.s0 + ci) * 4 + h
                                        vcopy(Sst[ci][:, 128:129], ppt[:, no:no + 1], [ppt], [Sst[ci]])
                                    yield
                                    for ci in range(nch):
                                        vcopy(Sbf[ci][:, 0:129], Sst[ci][:, 0:129], [Sst[ci]], [Sbf[ci]], eng=ACT_COPY)
                                        vcopy(nrep[ci][:], Sst[ci][:, 128:129].broadcast_to([128, 128]), [Sst[ci]], [nrep[ci]])
                                    yield
                                    for ci in range(nch):
                                        cs = slice(ci * c, (ci + 1) * c)
                                        mm(pnum[:, cs], Sbf[ci][:, 0:128], qi[:, cs], True, False, [Sbf[ci], qi], [pnum])
                                        mm(pnum[:, cs], v_tok[0:c, ci * 130:ci * 130 + 128], PTs[0:c, cs], False, True, [v_tok, PTs], [pnum])
                                        mm(pden[:, cs], nrep[ci][:], qi[:, cs], True, False, [nrep[ci], qi], [pden])
                                        mm(pden[:, cs], ones_bf[0:c, :], PTs[0:c, cs], False, True, [ones_bf, PTs], [pden])
                                    for b0 in range(0, nch, 3):
                                        pC = PS()
                                        nb = min(3, nch - b0)
                                        for j in range(nb):
                                            ci = b0 + j
                                            mm(pC[:, j * 129:(j + 1) * 129], kw_tok[0:c, ci * 128:(ci + 1) * 128], v_tok[0:c, ci * 130:ci * 130 + 129], True, True, [kw_tok, v_tok], [pC])
                                        for j in range(nb):
                                            ci = b0 + j
                                            ce = (ci + 1) * c - 1
                                            vstt(Sst[ci][:, 0:129], Sst[ci][:, 0:129], inter[:, ce:ce + 1], pC[:, j * 129:(j + 1) * 129], ALU.mult, ALU.add, [Sst[ci], inter, pC], [Sst[ci]])
                                        yield
                                    for ci in range(nch):
                                        ce = (ci + 1) * c - 1
                                        state_io('m', h, g, ci, ci, 128, 'store')
                                        so = 8 + l * 64 + (g.s0 + ci) * 4 + h
                                        vcopy(nst[:, so:so + 1], Sst[ci][:, 128:129], [Sst[ci]], [nst])
                                        vcopy(mst[0:1, so:so + 1], m[0:1, ce:ce + 1], [m], [mst])
                                    yield
                                for ci in (range(nch) if not g.samp else []):
                                    k = 0
                                    cs = slice(ci * c, (ci + 1) * c)
                                    mm(pnum[:, cs], Sbf[k][:, 0:128], qi[:, cs], True, False, [Sbf[k], qi], [pnum])
                                    mm(pnum[:, cs], v_tok[0:c, ci * 130:ci * 130 + 128], PTs[0:c, cs], False, True, [v_tok, PTs], [pnum])
                                    mm(pden[:, cs], nrep[k][:], qi[:, cs], True, False, [nrep[k], qi], [pden])
                                    mm(pden[:, cs], ones_bf[0:c, :], PTs[0:c, cs], False, True, [ones_bf, PTs], [pden])
                                    pC = PS()
                                    mm(pC[:, 0:129], kw_tok[0:c, ci * 128:(ci + 1) * 128], v_tok[0:c, ci * 130:ci * 130 + 129], True, True, [kw_tok, v_tok], [pC])
                                    ce = (ci + 1) * c - 1
                                    vstt(Sst[k][:, 0:129], Sst[k][:, 0:129], inter[:, ce:ce + 1], pC[:, 0:129], ALU.mult, ALU.add, [Sst[k], inter, pC], [Sst[k]])
                                    yield
                                    if g.samp or (g.idx == NPG - 1 and ci == nch - 1):
                                        state_io('m', h, g, ci, k, 128, 'store')
                                        yield
                                        so = (8 + l * 64 + (g.s0 + ci) * 4 + h) if g.samp else (l * 4 + h)
                                        vcopy(nst[:, so:so + 1], Sst[k][:, 128:129], [Sst[k]], [nst])
                                        yield
                                        vcopy(mst[0:1, so:so + 1], m[0:1, ce:ce + 1], [m], [mst])
                                        yield
                                    if not g.samp:
                                        vcopy(Sbf[k][:, 0:129], Sst[k][:, 0:129], [Sst[k]], [Sbf[k]], eng=ACT_COPY)
                                        yield
                                        vcopy(nrep[k][:], Sst[k][:, 128:129].broadcast_to([128, 128]), [Sst[k]], [nrep[k]])
                                        yield
                                vcopy(t1[:, 0:n], pden[:, 0:n], [pden], [t1])
                                yield
                                vstt(t1[:, 0:n], t1[:, 0:n], -1.0, t1[:, 0:n], ALU.mult, ALU.max, [t1], [t1])
                                yield
                                act(t3[:, 0:n], m[:, 0:n], AF.Exp, [m], [t3], scale=-1.0)
                                yield
                                vtt(t1[:, 0:n], t1[:, 0:n], t3[:, 0:n], ALU.max, [t1, t3], [t1])
                                yield
                                a_recip(t1[:, 0:n], t1[:, 0:n], [t1], t1)
                                yield
                                vtt(t3[:, 0:n], pnum[:, 0:n], t1[:, 0:n], ALU.mult, [pnum, t1], [t3])
                                yield
                                post(t3[:, 0:n], t3, pg, AF.Sigmoid, 'mnorm', h, g, 1)
                                yield

                        def gdn_head(h):
                            ws = load_head_w([O_GQKV + h * 128, O_GQKV + 512 + h * 128, O_GQKV + 1024 + h * 128, O_GZ + h * 128])
                            vset(Sst[0][:], 0.0, [Sst[0]])
                            vset(Sbf[0][:], 0.0, [Sbf[0]])
                            kw_tok, kbg_tok, bv_tok = tk[0], tk[1], tk[2]
                            for g in GRPS:
                                n, c, nch = g.ntok, g.c, g.nch
                                L = 6 if c == 64 else 2
                                if cix == 0 and h == 0 and l == 0:
                                    mark('  gdn g%d start' % g.idx)
                                rsm = cct[:, CC['rs256']:CC['rs256'] + n] if not g.samp else cct[:, CC['rs64']:CC['rs64'] + n]
                                beta, G, eG, wl, xs, r, kn, t1 = f[0], f[1], f[2], f[3], f[4], f[5], f[6], f[7]
                                qn, qg, knb, kb = b[0], b[1], b[2], b[3]
                                scalars(g)
                                yield
                                bcast(0, h, g, beta)
                                yield
                                bcast(1, h, g, t1)
                                yield
                                scan(G[:, 0:n], rsm, t1[:, 0:n], 0.0, ALU.mult, ALU.add, [cct, t1], [G])
                                yield
                                act(eG[:, 0:n], G[:, 0:n], AF.Exp, [G], [eG])
                                yield
                                lb_, Gv = lastbc(G, g)
                                vtt(wl[:, 0:n].rearrange("p (a c) -> p a c", c=c), lb_, Gv, ALU.subtract, [G], [wl])
                                yield
                                act(wl[:, 0:n], wl[:, 0:n], AF.Exp, [wl], [wl])
                                yield
                                if CFG.get('gstop', 99) <= 1:
                                    continue
                                def quantity(qi_, X, R):
                                    R_pre, R_xs, R_r, R_sq, R_bank, R_cso, R_gci, R_cst = R
                                    chunk = qi_ * 4 + h
                                    chunk = qi_ * 4 + h
                                    p = R_bank; proj(ws[qi_], g, p)
                                    yield
                                    if not g.samp:
                                        vcopy(R_pre[:, 3:3 + n], p[:, 0:n], [p], [R_pre], eng=ACT_E)
                                        yield
                                        if g.idx == 0:
                                            vset(R_pre[:, 0:3], 0.0, [R_pre])
                                        else:
                                            vcopy(R_pre[:, 0:3], hist[qi_][:, 0:3], [hist[qi_]], [R_pre], eng=POOL_E)
                                            yield
                                        vcopy(hist[qi_][:, 0:3], R_pre[:, n:n + 3], [R_pre], [hist[qi_]], eng=POOL_E)
                                        yield
                                        srcs = [R_pre[:, j:j + n] for j in range(4)]
                                        dst = R_xs[:, 0:n]
                                        if g.idx == NPG - 1:
                                            pt = R_bank
                                            tr(pt[0:3, 0:128], hist[qi_][:, 0:3], ident(128), [hist[qi_], cct], [pt])
                                            pass
                                            cst = R_cst
                                            vcopy(cst[0:3, :], pt[0:3, 0:128], [pt], [cst])
                                            yield
                                            S.dma(SP, pgc_o[l][:, chunk * 128:(chunk + 1) * 128], cst[0:3, :], R=[cst])
                                    else:
                                        pv = R_pre[:, 0:7 * nch].rearrange("p (s w) -> p s w", w=7)
                                        vcopy(pv[:, :, 3:7], p[:, 0:n].rearrange("p (s w) -> p s w", w=4), [p], [R_pre], eng=ACT_E)
                                        yield
                                        pt = R_bank
                                        pass
                                        gci = R_gci
                                        nr = 3 * nch
                                        S.dma(SP, gci[0:nr, :], gcs_d[l][g.s0 * 3:g.s0 * 3 + nr, chunk * 128:(chunk + 1) * 128], W=[gci])
                                        tr(pt[:, 0:nr], gci[0:nr, 0:128], ident(nr), [gci, cct], [pt])
                                        vcopy(pv[:, :, 0:3], pt[:, 0:nr].rearrange("p (s w) -> p s w", w=3), [pt], [R_pre])
                                        yield
                                        vcopy(R_cso[:, 0:nr].rearrange("p (s w) -> p s w", w=3), pv[:, :, 4:7], [R_pre], [R_cso])
                                        yield
                                        pt2 = R_bank
                                        tr(pt2[0:nr, 0:128], R_cso[:, 0:nr], ident(128), [R_cso, cct], [pt2])
                                        cst = R_cst
                                        vcopy(cst[0:nr, :], pt2[0:nr, 0:128], [pt2], [cst])
                                        yield
                                        S.dma(SP, sgc_o[l][g.s0 * 3:g.s0 * 3 + nr, chunk * 128:(chunk + 1) * 128], cst[0:nr, :], R=[cst])
                                        srcs = [pv[:, :, j:j + 4] for j in range(4)]
                                        dst = R_xs[:, 0:n].rearrange("p (s w) -> p s w", w=4)
                                    wo = PP['gconvw'] + (l * 12 + chunk) * 4
                                    if CFG.get('acttap', 0):
                                        psrc = p[:, 0:n] if not g.samp else p[:, 0:n].rearrange("p (s w) -> p s w", w=4)
                                        act(dst, psrc, AF.Identity, [p, ppt], [R_xs], scale=ppt[:, wo + 3:wo + 4])
                                        yield
                                        jr = range(0, 3)
                                    else:
                                        vts(dst, srcs[0], ppt[:, wo:wo + 1], None, ALU.mult, None, [R_pre, ppt], [R_xs])
                                        yield
                                        jr = range(1, 4)
                                    for j in jr:
                                        vstt(dst, srcs[j], ppt[:, wo + j:wo + j + 1], dst, ALU.mult, ALU.add, [R_pre, ppt, R_xs], [R_xs])
                                        yield
                                    a_sigmoid(R_r[:, 0:n], R_xs[:, 0:n], [R_xs], R_r)
                                    yield
                                    vtt(R_xs[:, 0:n], R_xs[:, 0:n], R_r[:, 0:n], ALU.mult, [R_xs, R_r], [R_xs])
                                    yield
                                    if X in 'qk':
                                        act(R_sq[:, 0:n], R_xs[:, 0:n], AF.Square, [R_xs], [R_sq])
                                        yield
                                        p2 = R_bank
                                        mm(p2[:, 0:n], ones_bf[:], R_sq[:, 0:n], True, True, [ones_bf, R_sq], [p2])
                                        a_rsqrt(R_r[:, 0:n], p2[:, 0:n], 1.0, [p2], R_r)
                                        yield
                                    if X == 'q':
                                        vstt(qn[:, 0:n], R_xs[:, 0:n], 128.0 ** -0.5, R_r[:, 0:n], ALU.mult, ALU.mult, [R_xs, R_r], [qn])
                                        yield
                                        vtt(qg[:, 0:n], qn[:, 0:n], eG[:, 0:n], ALU.mult, [qn, eG], [qg])
                                        yield
                                    elif X == 'k':
                                        vtt(kn[:, 0:n], R_xs[:, 0:n], R_r[:, 0:n], ALU.mult, [R_xs, R_r], [kn])
                                        yield
                                        vcopy(knb[:, 0:n], kn[:, 0:n], [kn], [knb])
                                        yield
                                        vtt(kb[:, 0:n], kn[:, 0:n], beta[:, 0:n], ALU.mult, [kn, beta], [kb])
                                        yield
                                        vtt(t1[:, 0:n], kn[:, 0:n], wl[:, 0:n], ALU.mult, [kn, wl], [t1])
                                        yield
                                        to_tok(t1, g, kw_tok, bank=R_bank)
                                        yield
                                        vtt(t1[:, 0:n], kn[:, 0:n], beta[:, 0:n], ALU.mult, [kn, beta], [t1])
                                        yield
                                        vtt(t1[:, 0:n], t1[:, 0:n], eG[:, 0:n], ALU.mult, [t1, eG], [t1])
                                        yield
                                        to_tok(t1, g, kbg_tok, bank=R_bank)
                                        yield
                                    else:
                                        vtt(R_xs[:, 0:n], R_xs[:, 0:n], beta[:, 0:n], ALU.mult, [R_xs, beta], [R_xs])
                                        yield
                                        to_tok(R_xs, g, bv_tok, bank=R_bank)
                                        yield
                                RES = [(pre, f[4], f[5], b[7], psb[4 + 2 * cix], cso, gcis[0], csts[0]),
                                       (pre2, f[8], f[9], b[6], psb[5 + 2 * cix], cso2, gcis[1], csts[1])]

                                def qv_chain():
                                    yield from quantity(0, 'q', RES[0])
                                    yield from quantity(2, 'v', RES[0])
                                fibs = [qv_chain(), quantity(1, 'k', RES[1])]
                                while fibs:
                                    for fb in list(fibs):
                                        try:
                                            next(fb)
                                        except StopIteration:
                                            fibs.remove(fb)
                                        yield
                                if CFG.get('gstop', 99) <= 2:
                                    continue
                                if cix == 0 and h == 0 and l == 0:
                                    mark('  gdn g%d qkv done' % g.idx)
                                pg = PSA(); proj(ws[3], g, pg)
                                vcopy(f[9][:, 0:n], pg[:, 0:n], [pg], [f[9]], eng=ACT_E)
                                pg = f[9]
                                yield
                                for b0 in range(0, nch, 4):
                                    p = PS()
                                    for j in range(4):
                                        ci = b0 + j
                                        tr(p[0:c, j * 128:(j + 1) * 128], G[:, ci * c:(ci + 1) * c], ident(128), [G, cct], [p])
                                    vcopy(cols[0:c, b0:b0 + 4], p[0:c, 0:512].rearrange("p (a w) -> p a w", a=4)[:, :, 0], [p], [cols])
                                    yield
                                DT, Dm = f[5], f[7]
                                for ci in range(nch):
                                    cs = slice(ci * c, (ci + 1) * c)
                                    vts(DT[0:c, cs], G[0:c, cs], cols[0:c, ci:ci + 1], 0.0, ALU.subtract, ALU.min, [G, cols], [DT])
                                    yield
                                    vts(Dm[0:c, cs], G[0:c, cs], -1.0, cols[0:c, ci:ci + 1], ALU.mult, ALU.add, [G, cols], [Dm])
                                    yield
                                vts(Dm[0:c, 0:n], Dm[0:c, 0:n], 0.0, None, ALU.min, None, [Dm], [Dm])
                                yield
                                act(DT[0:c, 0:n], DT[0:c, 0:n], AF.Exp, [DT], [DT])
                                yield
                                act(Dm[0:c, 0:n], Dm[0:c, 0:n], AF.Exp, [Dm], [Dm])
                                yield
                                vtt(Dm[0:c, 0:n], Dm[0:c, 0:n], cmask('mS', g), ALU.mult, [Dm, mtile(g)], [Dm], eng=POOL_E)
                                yield
                                DTs = f[6]
                                vtt(DTs[0:c, 0:n], DT[0:c, 0:n], cmask('mTs', g), ALU.mult, [DT, mtile(g)], [DTs], eng=POOL_E)
                                yield
                                vtt(DT[0:c, 0:n], DT[0:c, 0:n], cmask('mTc', g), ALU.mult, [DT, mtile(g)], [DT], eng=POOL_E)
                                yield
                                if CFG.get('gstop', 99) <= 3:
                                    continue
                                if cix == 0 and h == 0 and l == 0:
                                    mark('  gdn g%d decay done' % g.idx)
                                Xa, Ya, Ra = [b[4], b[5]], [b[6], b[7]], [tk[2], None]
                                Ra = [S_R[0], S_R[1]]
                                px = PS(); py = PS()
                                for ci in range(nch):
                                    cs = slice(ci * c, (ci + 1) * c)
                                    mm(py[0:c, cs], kb[:, cs], knb[:, cs], True, True, [kb, knb], [py])
                                    mm(px[0:c, cs], knb[:, cs], kb[:, cs], True, True, [kb, knb], [px])
                                vstt(Ya[0][0:c, 0:n], py[0:c, 0:n], -1.0, Dm[0:c, 0:n], ALU.mult, ALU.mult, [py, Dm], [Ya[0]])
                                yield
                                vstt(Xa[0][0:c, 0:n], px[0:c, 0:n], -1.0, DTs[0:c, 0:n], ALU.mult, ALU.mult, [px, DTs], [Xa[0]])
                                yield
                                vtt(Ra[0][0:c, 0:n], Xa[0][0:c, 0:n], cmask('id', g), ALU.add, [Xa[0], mtile(g)], [Ra[0]])
                                yield
                                cur = 0
                                if CFG.get('gstop', 99) <= 4:
                                    continue
                                for lev in range(1, min(L, CFG.get('glev', 99))):
                                    nx = 1 - cur
                                    px = PS(); py = PS()
                                    for ci in range(nch):
                                        cs = slice(ci * c, (ci + 1) * c)
                                        mm(px[0:c, cs], Ya[cur][0:c, cs], Xa[cur][0:c, cs], True, True, [Ya[cur], Xa[cur]], [px])
                                        mm(py[0:c, cs], Xa[cur][0:c, cs], Ya[cur][0:c, cs], True, True, [Ya[cur], Xa[cur]], [py])
                                    vcopy(Xa[nx][0:c, 0:n], px[0:c, 0:n], [px], [Xa[nx]])
                                    yield
                                    vcopy(Ya[nx][0:c, 0:n], py[0:c, 0:n], [py], [Ya[nx]], eng=ACT_COPY)
                                    yield
                                    pr = PS()
                                    for ci in range(nch):
                                        cs = slice(ci * c, (ci + 1) * c)
                                        mm(pr[0:c, cs], Ya[nx][0:c, cs], Ra[cur][0:c, cs], True, True, [Ya[nx], Ra[cur]], [pr])
                                    vtt(Ra[nx][0:c, 0:n], Ra[cur][0:c, 0:n], pr[0:c, 0:n], ALU.add, [Ra[cur], pr], [Ra[nx]])
                                    yield
                                    cur = nx
                                if CFG.get('gstop', 99) <= 5:
                                    continue
                                if cix == 0 and h == 0 and l == 0:
                                    mark('  gdn g%d inverse done' % g.idx)
                                TTm = Ra[cur]
                                nW = b[4] if cur == 1 else b[5]
                                nW = Xa[1 - cur]
                                pw = PS()
                                for ci in range(nch):
                                    cs = slice(ci * c, (ci + 1) * c)
                                    mm(pw[:, cs], kbg_tok[0:c, ci * 128:(ci + 1) * 128], TTm[0:c, cs], True, True, [kbg_tok, TTm], [pw])
                                vts(nW[:, 0:n], pw[:, 0:n], -1.0, None, ALU.mult, None, [pw], [nW])
                                yield
                                for b0 in range(0, nch, 4):
                                    puv = PS()
                                    for j in range(4):
                                        ci = b0 + j
                                        cs = slice(ci * c, (ci + 1) * c)
                                        mm(puv[0:c, j * 128:(j + 1) * 128], TTm[0:c, cs], bv_tok[0:c, ci * 128:(ci + 1) * 128], True, True, [TTm, bv_tok], [puv])
                                    vcopy(uvt[0:c, b0 * 128:(b0 + 4) * 128], puv[0:c, 0:512], [puv], [uvt], eng=ACT_E)
                                    yield
                                pq_ = PS()
                                for ci in range(nch):
                                    cs = slice(ci * c, (ci + 1) * c)
                                    mm(pq_[0:c, cs], knb[:, cs], qn[:, cs], True, True, [knb, qn], [pq_])
                                QTs = Ya[1 - cur]
                                vtt(QTs[0:c, 0:n], pq_[0:c, 0:n], DT[0:c, 0:n], ALU.mult, [pq_, DT], [QTs])
                                yield
                                if CFG.get('gstop', 99) <= 6:
                                    continue
                                if cix == 0 and h == 0 and l == 0:
                                    mark('  gdn g%d seq start' % g.idx)
                                po = PSA()
                                if g.samp:
                                    for ci in range(nch):
                                        state_io('g', h, g, ci, ci, 128, 'load')
                                    yield
                                    for ci in range(nch):
                                        vcopy(Sbf[ci][:, 0:128], Sst[ci][:, 0:128], [Sst[ci]], [Sbf[ci]], eng=ACT_COPY)
                                    yield
                                    for b0 in range(0, nch, 4):
                                        pu = PS()
                                        for j in range(4):
                                            ci = b0 + j
                                            cs = slice(ci * c, (ci + 1) * c)
                                            mm(pu[0:c, j * 128:(j + 1) * 128], nW[:, cs], Sbf[ci][:, 0:128], True, True, [nW, Sbf[ci]], [pu])
                                        vtt(ubig[0:c, b0 * 128:(b0 + 4) * 128], pu[0:c, 0:512], uvt[0:c, b0 * 128:(b0 + 4) * 128], ALU.add, [pu, uvt], [ubig])
                                        yield
                                    for ci in range(nch):
                                        cs = slice(ci * c, (ci + 1) * c)
                                        mm(po[:, cs], Sbf[ci][:, 0:128], qg[:, cs], True, False, [Sbf[ci], qg], [po])
                                        mm(po[:, cs], ubig[0:c, ci * 128:(ci + 1) * 128], QTs[0:c, cs], False, True, [ubig, QTs], [po])
                                    for b0 in range(0, nch, 4):
                                        pS_ = PS()
                                        for j in range(4):
                                            ci = b0 + j
                                            mm(pS_[:, j * 128:(j + 1) * 128], kw_tok[0:c, ci * 128:(ci + 1) * 128], ubig[0:c, ci * 128:(ci + 1) * 128], True, True, [kw_tok, ubig], [pS_])
                                        for j in range(4):
                                            ci = b0 + j
                                            ce = (ci + 1) * c - 1
                                            vstt(Sst[ci][:, 0:128], Sst[ci][:, 0:128], eG[:, ce:ce + 1], pS_[:, j * 128:(j + 1) * 128], ALU.mult, ALU.add, [Sst[ci], eG, pS_], [Sst[ci]])
                                        yield
                                    for ci in range(nch):
                                        state_io('g', h, g, ci, ci, 128, 'store')
                                    yield
                                for ci in (range(nch) if not g.samp else []):
                                    k = 0
                                    cs = slice(ci * c, (ci + 1) * c)
                                    pu = PS()
                                    if CFG.get('dbg', 0) == 1:
                                        mm(pu[0:c, 0:128], TTm[0:c, cs], bv_tok[0:c, ci * 128:(ci + 1) * 128], True, True, [TTm, bv_tok], [pu])
                                    else:
                                        mm(pu[0:c, 0:128], nW[:, cs], Sbf[k][:, 0:128], True, True, [nW, Sbf[k]], [pu])
                                    uo = (ci % 2) * 128
                                    vtt(ubig[0:c, uo:uo + 128], pu[0:c, 0:128], uvt[0:c, ci * 128:(ci + 1) * 128], ALU.add, [pu, uvt], [ubig])
                                    yield
                                    if CFG.get('dbg', 0) == 2:
                                        continue
                                    mm(po[:, cs], Sbf[k][:, 0:128], qg[:, cs], True, False, [Sbf[k], qg], [po])
                                    if CFG.get('dbg', 0) == 5:
                                        mm(po[:, cs], kw_tok[0:c, ci * 128:(ci + 1) * 128], QTs[0:c, cs], False, True, [kw_tok, QTs], [po])
                                    else:
                                        mm(po[:, cs], ubig[0:c, uo:uo + 128], QTs[0:c, cs], False, True, [ubig, QTs], [po])
                                    if CFG.get('dbg', 0) == 3:
                                        continue
                                    pS_ = PS()
                                    mm(pS_[:, 0:128], kw_tok[0:c, ci * 128:(ci + 1) * 128], ubig[0:c, uo:uo + 128], True, True, [kw_tok, ubig], [pS_])
                                    ce = (ci + 1) * c - 1
                                    vstt(Sst[k][:, 0:128], Sst[k][:, 0:128], eG[:, ce:ce + 1], pS_[:, 0:128], ALU.mult, ALU.add, [Sst[k], eG, pS_], [Sst[k]])
                                    yield
                                    if g.samp or (g.idx == NPG - 1 and ci == nch - 1):
                                        state_io('g', h, g, ci, k, 128, 'store')
                                        yield
                                    if not g.samp:
                                        vcopy(Sbf[k][:, 0:128], Sst[k][:, 0:128], [Sst[k]], [Sbf[k]], eng=ACT_COPY)
                                        yield
                                if CFG.get('dbg', 0) in (2, 3, 4):
                                    continue
                                if cix == 0 and h == 0 and l == 0:
                                    mark('  gdn g%d seq done' % g.idx)
                                post(po[:, 0:n], po, pg, AF.Silu, 'gnorm', h, g, 0)
                                yield

                        S_R = [TF("Ra%d" % i, [64, NT], BF16) for i in range(2)]
                        return (gdn_head, mlstm_head, hgrn_head)

                    ctxs = [make_ctx(i) for i in range(2)]


                    for h0 in range(0, CFG['heads'], 2):
                        cols_ = ([O_GQKV, O_GQKV + 512, O_GQKV + 1024, O_GZ], [O_MQ, O_MK, O_MV, O_MO], [O_HQ, O_HF, O_HI, O_HG])[bi]
                        for i_, o_ in enumerate(cols_):
                            o2 = o_ + h0 * 128
                            S.dma(POOL, hwp[i_][:], w_in_d[l][:, o2:o2 + 256].rearrange("(kc p) j -> p kc j", p=128), W=[hwp[i_]])
                        gens = [ctxs[j][bi](h0 + j) for j in range(2) if h0 + j < CFG['heads']]
                        if not CFG.get('lsched', 1):
                            while gens:
                                for gnr in list(gens):
                                    try:
                                        next(gnr)
                                    except StopIteration:
                                        gens.remove(gnr)
                            continue
                        qs = [[] for _ in gens]
                        rr = [0]
                        alive = [True] * len(gens)
                        while True:
                            for k_, gnr in enumerate(gens):
                                while alive[k_] and len(qs[k_]) < 128:
                                    S.capture = qs[k_]
                                    try:
                                        next(gnr)
                                    except StopIteration:
                                        alive[k_] = False
                                    S.capture = None
                            cands = [k_ for k_ in range(len(gens)) if qs[k_]]
                            if not cands:
                                break
                            if CFG.get('lsched', 1) == 2 or bi not in CFG.get('lsm', (0, 1, 2)):
                                rr[0] += 1
                                kb = cands[rr[0] % len(cands)]
                            else:
                                W_ = CFG.get('win', 48)
                                if W_ <= 1:
                                    kb = min(cands, key=lambda k_: S.est_start(qs[k_][0]))
                                    S.emit(qs[kb].pop(0))
                                    continue
                                best = None
                                for k_ in cands:
                                    seen_r, seen_w = set(), set()
                                    for pos, it in enumerate(qs[k_][:W_]):
                                        r_ = set(id(t_) for t_ in it[3])
                                        w_ = set(id(t_) for t_ in it[4])
                                        indep = not (w_ & (seen_r | seen_w)) and not (r_ & seen_w)
                                        if indep:
                                            es = S.est_start(it) + 0.02 * pos
                                            if best is None or es < best[0]:
                                                best = (es, k_, pos)
                                        seen_r |= r_
                                        seen_w |= w_
                                S.emit(qs[best[1]].pop(best[2]))
                                continue
                            S.emit(qs[kb].pop(0))
                    S.barrier()
                    S.release(cm)
                mark('L%d merge%d' % (l, bi))
                if not CFG['merge']:
                    continue
                if bi == 2:
                    x_alloc()
                    for i in range(5):
                        if l == 0:
                            o_ = sum(TT_N[:i])
                            S.dma(SP, xT[i][:], xT_d[:, o_:o_ + TT_N[i]].rearrange("(kc p) t -> p kc t", p=128), W=[xT[i]])
                        else:
                            S.dma(SP, xT[i][:], xs_view(i), W=[xT[i]])
                with ExitStack() as cg:
                    merged = S.sb("merged", [128, 8, T], BF16, cg)
                    mgt = [S.sb("mgt%d" % i, [128, 512], F32, cg) for i in range(2)]
                    mview = mscr.rearrange("p (k t) -> p k t", k=8)
                    if bi > 0:
                        for k2 in range(0, 8, 2):
                            S.dma(SP, merged[:, k2:k2 + 2, :], mview[:, k2:k2 + 2, :], W=[merged])
                    def merge(bi):
                        def ldm(dc):
                            wg_ = loadw(w_in_d[l][:, O_BR + bi * D + dc * 128:O_BR + bi * D + (dc + 1) * 128])
                            wqi[0] = (wqi[0] + 1) % NWQ
                            wb2 = wq[wqi[0]]
                            S.dma(POOL, wb2[:, 0:4, :], w_br_d[l, bi][:, dc * 128:(dc + 1) * 128].rearrange("(h p) j -> p h j", p=128), W=[wb2])
                            return wg_, wb2
                        nxt = ldm(0)
                        for dc in range(KC):
                            wg, wb_ = nxt
                            if dc + 1 < KC:
                                nxt = ldm(dc + 1)
                            for tt in range(5):
                                n = TT_N[tt]
                                o = sum(TT_N[:tt])
                                pgt = PS()
                                for kc in range(KC):
                                    mm(pgt[:, 0:n], wg[:, kc, :], hT[tt][:, kc, :], kc == 0, kc == KC - 1, [wg, hT[tt]], [pgt])
                                pb = PS()
                                for hh in range(4):
                                    mm(pb[:, 0:n], wb_[:, hh, :], ob[:, hh, o:o + n], hh == 0, hh == 3, [wb_, ob], [pb])
                                gt = mgt[tt % 2]
                                act(gt[:, 0:n], pgt[:, 0:n], AF.Sigmoid, [pgt], [gt])
                                if bi == 0:
                                    vtt(merged[:, dc, o:o + n], gt[:, 0:n], pb[:, 0:n], ALU.mult, [gt, pb], [merged])
                                else:
                                    vtt(gt[:, 0:n], gt[:, 0:n], pb[:, 0:n], ALU.mult, [gt, pb], [gt])
                                    vtt(merged[:, dc, o:o + n], merged[:, dc, o:o + n], gt[:, 0:n], ALU.add, [gt, merged], [merged])

                    mq = []
                    S.capture = mq
                    merge(bi)
                    if bi < 2:
                        for k2 in range(0, 8, 2):
                            S.dma(SP, mview[:, k2:k2 + 2, :], merged[:, k2:k2 + 2, :], R=[merged])
                    else:
                        mark('L%d wout' % l)
                        nxw = loadw(w_out_d[l][:, 0:128])
                        for dc in range(KC):
                            w = nxw
                            if dc + 1 < KC:
                                nxw = loadw(w_out_d[l][:, (dc + 1) * 128:(dc + 2) * 128])
                            for tt in range(5):
                                n = TT_N[tt]
                                o = sum(TT_N[:tt])
                                p = PS()
                                for kc in range(KC):
                                    mm(p[:, 0:n], w[:, kc, :], merged[:, kc, o:o + n], kc == 0, kc == KC - 1, [w, merged], [p])
                                vtt(xT[tt][:, dc, :], xT[tt][:, dc, :], p[:, 0:n], ALU.add, [xT[tt], p], [xT[tt]])
                    S.capture = None
                    S.run_window(mq, CFG.get('mwin', 48))
                    S.barrier()
                    S.release(cg)
            if not CFG['merge']:
                x_alloc()
                for i in range(5):
                    S.dma(SP, xT[i][:], xs_view(i), W=[xT[i]])

            mark('L%d rmsnorm2' % l)
            rmsnorm(hT, 'lnffn', l * 8)
            mark('L%d ffn' % l)
            with ExitStack() as cf:
                ua2 = [[S.sb("ua%d%d" % (i, k), [128, 2 + 512], F32, cf) for k in range(2)] for i in range(2)]
                cv2 = [[S.sb("cv%d%d" % (i, k), [128, 512], F32, cf) for k in range(2)] for i in range(2)]
                NJ = 6
                wd = S.sb("wd", [128, NJ, D], BF16, cf)
                fit = [0]
                fcis = [S.sb("fci%d" % i, [32, 128], F32, cf) for i in range(2)]
                fsts = [S.sb("fst%d" % i, [32, 128], F32, cf) for i in range(2)]
                fio = [0]
                fh = [[S.sb("fh%d%d" % (i, j), [128, 2], F32, cf) for j in range(2)] for i in range(2)]
                fcs2 = [S.sb("fcso%d" % i, [128, 32], F32, cf) for i in range(2)]
                gbuf = S.sb("gbuf", [128, NJ, T], BF16, cf)
                NPASS = 4 if CFG['ffn'] else 0
                JS = [0, 6, 12, 17, 22]
                def ldu(j):
                    return [loadw(w_up_d[l][:, (half * 22 + j) * 128:(half * 22 + j + 1) * 128]) for half in range(2)]
                ffq = []
                if CFG.get('ffwin', 64) > 1:
                    S.capture = ffq
                nxu = ldu(0) if NPASS else None
                pend = []
                for ps_ in range(NPASS):
                    j0 = JS[ps_]
                    nj = JS[ps_ + 1] - j0
                    S.dma(POOL, wd[:, 0:nj, :], w_dn_d[l][j0 * 128:(j0 + nj) * 128, :].rearrange("(j p) d -> p j d", p=128), W=[wd])
                    for jj in range(nj):
                        j = j0 + jj
                        wts = nxu
                        if j + 1 < 22:
                            nxu = ldu(j + 1)
                        dq = []
                        for half in range(2):
                            fc = half * 22 + j
                            S.dma(SP, fcis[half][:], fcs_d[l][:, fc * 128:(fc + 1) * 128], W=[fcis[half]])
                        for tt in range(5):
                            n = TT_N[tt]
                            o = sum(TT_N[:tt])
                            fit[0] += 1
                            ua = [ua2[0][fit[0] % 2], ua2[1][fit[0] % 2]]
                            cv = [cv2[0][fit[0] % 2], cv2[1][fit[0] % 2]]
                            for half in range(2):
                                fc = half * 22 + j
                                u_ = ua[half]
                                p = PS()
                                for kc in range(KC):
                                    mm(p[:, 0:n], wts[half][:, kc, :], hT[tt][:, kc, :], kc == 0, kc == KC - 1, [wts[half], hT[tt]], [p])
                                wo = PP['fconvw'] + (l * NFC + fc) * 3
                                bo = PP['fconvb'] + l * NFC + fc
                                if tt < 4:
                                    vcopy(u_[:, 2:2 + n], p[:, 0:n], [p], [u_], eng=ACT_COPY)
                                    if tt == 0:
                                        vset(u_[:, 0:2], 0.0, [u_])
                                    else:
                                        vcopy(u_[:, 0:2], fh[half][0][:, 0:2], [fh[half][0]], [u_], eng=POOL_E)
                                    vcopy(fh[half][0][:, 0:2], u_[:, n:n + 2], [u_], [fh[half][0]], eng=POOL_E)
                                    if tt == 3:
                                        vcopy(fh[half][1][:, 0:2], u_[:, n:n + 2], [u_], [fh[half][1]], eng=POOL_E)

                                        def _pst(half=half, fc=fc):
                                            pt = PS()
                                            S.op(PE, lambda e: e.transpose(pt[0:2, 0:128], fh[half][1][:, 0:2], ident(128)), [fh[half][1], cct], [pt])
                                            fst = fsts[half]
                                            vcopy(fst[0:2, :], pt[0:2, 0:128], [pt], [fst])
                                            S.dma(SP, pfc_o[l][:, fc * 128:(fc + 1) * 128], fst[0:2, :], R=[fst])
                                        dq.append(_pst)
                                    srcs = [u_[:, jx:jx + n] for jx in range(3)]
                                    dst = cv[half][:, 0:n]
                                else:
                                    uv = u_[:, 0:96].rearrange("p (s w) -> p s w", w=6)
                                    vcopy(uv[:, :, 2:6], p[:, 0:n].rearrange("p (s w) -> p s w", w=4), [p], [u_])
                                    pt = PS()
                                    fci = fcis[half]
                                    S.op(PE, (lambda pt=pt, fci=fci: (lambda e: e.transpose(pt[:, 0:32], fci[0:32, 0:128], ident(32))))(), [fci, cct], [pt])
                                    vcopy(uv[:, :, 0:2], pt[:, 0:32].rearrange("p (s w) -> p s w", w=2), [pt], [u_])
                                    fcsh = fcs2[half]
                                    vcopy(fcsh[:, 0:32].rearrange("p (s w) -> p s w", w=2), uv[:, :, 4:6], [u_], [fcsh], eng=POOL_E)

                                    def _sst(half=half, fc=fc, fcsh=fcsh):
                                        pt2 = PS()
                                        S.op(PE, lambda e: e.transpose(pt2[0:32, 0:128], fcsh[:, 0:32], ident(128)), [fcsh, cct], [pt2])
                                        fst = fsts[half]
                                        vcopy(fst[0:32, :], pt2[0:32, 0:128], [pt2], [fst])
                                        S.dma(SP, sfc_o[l][:, fc * 128:(fc + 1) * 128], fst[0:32, :], R=[fst])
                                    dq.append(_sst)
                                    srcs = [uv[:, :, jx:jx + 4] for jx in range(3)]
                                    dst = cv[half][:, 0:n].rearrange("p (s w) -> p s w", w=4)
                                if CFG.get('acttap', 0):
                                    psrc = p[:, 0:n] if tt < 4 else p[:, 0:n].rearrange("p (s w) -> p s w", w=4)
                                    act(dst, psrc, AF.Identity, [p, ppt], [cv[half]], scale=ppt[:, wo + 2:wo + 3], bias=ppt[:, bo:bo + 1])
                                    jr = range(0, 2)
                                else:
                                    vts(dst, srcs[0], ppt[:, wo:wo + 1], ppt[:, bo:bo + 1], ALU.mult, ALU.add, [u_, ppt], [cv[half]], eng=POOL_E)
                                    jr = range(1, 3)
                                for jx in jr:
                                    vstt(dst, srcs[jx], ppt[:, wo + jx:wo + jx + 1], dst, ALU.mult, ALU.add, [u_, ppt, cv[half]], [cv[half]])
                            act(cv[0][:, 0:n], cv[0][:, 0:n], AF.Silu, [cv[0]], [cv[0]])
                            vtt(gbuf[:, jj, o:o + n], cv[0][:, 0:n], cv[1][:, 0:n], ALU.mult, [cv[0], cv[1]], [gbuf])
                            if tt == 0:
                                for fn_ in pend:
                                    fn_()
                                pend = []
                        pend = dq
                    for dc in range(KC):
                        for tt in range(5):
                            n = TT_N[tt]
                            o = sum(TT_N[:tt])
                            p = PS()
                            for jj in range(nj):
                                mm(p[:, 0:n], wd[:, jj, dc * 128:(dc + 1) * 128], gbuf[:, jj, o:o + n], jj == 0, jj == nj - 1, [wd, gbuf], [p])
                            vtt(xT[tt][:, dc, :], xT[tt][:, dc, :], p[:, 0:n], ALU.add, [xT[tt], p], [xT[tt]])
                for fn_ in pend:
                    fn_()
                S.capture = None
                S.run_window(ffq, CFG.get('ffwin', 64))
                S.barrier()
                S.release(cf)

        mark('final')
        rmsnorm(None, 'lnfin', 0, store=yT_d)
        x_free()
        S.dma(SP, nst_o, nst[:], R=[nst])
        S.dma(SP, mst_o, mst[:], R=[mst])
        S.barrier()
        _NC_CACHE['stats'] = (dict(S.seq), len(S.groups), max(g.cnt for g in S.groups))
    return nc


_NC_CACHE = {}


def kernel(**inp):
    f32 = np.float32
    x_prompt = np.asarray(inp['x_prompt'], f32)
    x_sample = np.asarray(inp['x_sample'], f32)
    cc = build_consts()

    def part(a, nch):
        return np.ascontiguousarray(np.asarray(a, f32).reshape(nch, 128).T)

    pp0 = np.zeros((128, NPP), f32)
    for l in range(DEPTH):
        pp0[:, PP['lnmix'] + l * 8:PP['lnmix'] + (l + 1) * 8] = part(inp['ln_mix'][l], 8)
        pp0[:, PP['lnffn'] + l * 8:PP['lnffn'] + (l + 1) * 8] = part(inp['ln_ffn'][l], 8)
        gw = np.asarray(inp['gdn_conv_w'][l], f32)
        pp0[:, PP['gconvw'] + l * 48:PP['gconvw'] + (l + 1) * 48] = gw.T.reshape(12, 128, 4).transpose(1, 0, 2).reshape(128, 48)
        for nm, key in (('gnorm', 'gdn_norm'), ('mnorm', 'm_norm'), ('hnorm', 'hgrn_norm'), ('hlb', 'hgrn_lb')):
            pp0[:, PP[nm] + l * 4:PP[nm] + (l + 1) * 4] = part(inp[key][l], 4)
        pp0[32:36, PP['scpar'] + l * 2] = np.asarray(inp['gdn_dt_bias'][l], f32)
        pp0[64:68, PP['scpar'] + l * 2] = np.asarray(inp['m_ibias'][l], f32)
        pp0[96:100, PP['scpar'] + l * 2] = np.asarray(inp['m_fbias'][l], f32)
        pp0[32:36, PP['scpar'] + l * 2 + 1] = np.asarray(inp['gdn_A_log'][l], f32)
        fw = np.asarray(inp['ffn_conv_w'][l], f32)
        pp0[:, PP['fconvw'] + l * NFC * 3:PP['fconvw'] + (l + 1) * NFC * 3] = fw.T.reshape(NFC, 128, 3).transpose(1, 0, 2).reshape(128, NFC * 3)
        pp0[:, PP['fconvb'] + l * NFC:PP['fconvb'] + (l + 1) * NFC] = part(inp['ffn_conv_b'][l], NFC)
    pp0[:, PP['lnfin']:PP['lnfin'] + 8] = part(inp['ln_final'], 8)

    w_in = np.ascontiguousarray(inp['w_in'], f32)
    w_br = np.ascontiguousarray(inp['w_br'], f32)
    w_out = np.ascontiguousarray(inp['w_out'], f32)
    w_up = np.ascontiguousarray(inp['w_up'], f32)
    w_down = np.ascontiguousarray(inp['w_down'], f32)
    in_maps = []
    for c in range(8):
        sl = slice(c * NSQ, (c + 1) * NSQ)
        xs = np.concatenate([x_prompt[c], x_sample[sl].reshape(NSQ * 4, D)], axis=0)
        pp = pp0.copy()
        m_in = np.asarray(inp['state_mlstm_m'], f32)[:, sl]
        pp[:, PP['min']:PP['min'] + 128] = m_in.reshape(1, 128)
        n_in = np.asarray(inp['state_mlstm_n'], f32)[:, sl]
        pp[:, PP['nin']:PP['nin'] + 128] = n_in.reshape(128, 128).T
        m = {"xT": np.ascontiguousarray(xs.T), "w_in": w_in, "w_br": w_br, "w_out": w_out, "w_up": w_up,
             "w_down": w_down, "pp": pp, "cc": cc,
             "sg_S": np.ascontiguousarray(inp['state_gdn_S'][:, sl], f32),
             "sm_S": np.ascontiguousarray(inp['state_mlstm_C'][:, sl], f32),
             "sh_S": np.ascontiguousarray(inp['state_hgrn_S'][:, sl], f32),
             "gcs": np.ascontiguousarray(np.asarray(inp['state_gdn_conv'], f32)[:, sl].reshape(DEPTH, NSQ * 3, 1536)),
             "fcs": np.ascontiguousarray(np.asarray(inp['state_ffn_conv'], f32)[:, sl].reshape(DEPTH, NSQ * 2, 2 * DFF))}
        in_maps.append(m)
    if 'nc' not in _NC_CACHE:
        _NC_CACHE['nc'] = build_nc()
    res = run_bass_kernel_spmd(_NC_CACHE['nc'], in_maps, core_ids=list(range(8)))
    R = res.results
    yT = [np.asarray(r["yT"]) for r in R]
    y_prompt = np.stack([y[:, :TP].T for y in yT], 0)
    y_sample = np.concatenate([y[:, TP:].T.reshape(NSQ, 4, D) for y in yT], 0)

    def pst(key):
        return np.stack([np.asarray(r[key]) for r in R], 1)

    def sst(key):
        return np.concatenate([np.asarray(r[key]) for r in R], 1)

    p_gdn_S, p_mC, p_hS = pst("pg_S"), pst("pm_S"), pst("ph_S")
    s_gdn_S, s_mC, s_hS = sst("og_S"), sst("om_S"), sst("oh_S")
    p_gconv = np.stack([np.asarray(r["pgc"]) for r in R], 1)
    s_gconv = np.concatenate([np.asarray(r["sgc"]).reshape(DEPTH, NSQ, 3, 1536) for r in R], 1)
    p_fconv = np.stack([np.asarray(r["pfc"]) for r in R], 1)
    s_fconv = np.concatenate([np.asarray(r["sfc"]).reshape(DEPTH, NSQ, 2, 2 * DFF) for r in R], 1)
    nst = [np.asarray(r["nst"]) for r in R]
    mst = [np.asarray(r["mst"]) for r in R]
    p_n = np.stack([n_[:, 0:8].T.reshape(DEPTH, 4, 128) for n_ in nst], 1)
    s_n = np.concatenate([n_[:, 8:].T.reshape(DEPTH, NSQ, 4, 128) for n_ in nst], 1)
    p_m = np.stack([m_[0, 0:8].reshape(DEPTH, 4) for m_ in mst], 1)
    s_m = np.concatenate([m_[0, 8:].reshape(DEPTH, NSQ, 4) for m_ in mst], 1)
    outs = (y_prompt, y_sample, p_gdn_S, p_gconv, p_mC, p_n, p_m, p_hS, p_fconv,
            s_gdn_S, s_gconv, s_mC, s_n, s_m, s_hS, s_fconv)
    return tuple(np.ascontiguousarray(o, dtype=f32) for o in outs)
```

```python
from contextlib import ExitStack
import math
import numpy as np
import concourse.bass as bass
import concourse.mybir as mybir
from concourse.bass_utils import run_bass_kernel_spmd

F32 = mybir.dt.float32
BF16 = mybir.dt.bfloat16
ALU = mybir.AluOpType
AF = mybir.ActivationFunctionType

PE, DVE, ACT, POOL, SP = 'tensor', 'vector', 'scalar', 'gpsimd', 'sync'
ENGS = (PE, DVE, ACT, POOL, SP)


HOP = 0.35


class DmaGroup:
    def __init__(self, sem):
        self.sem = sem
        self.cnt = 0


class Tile:
    def __init__(self, name, t, dg=None):
        self.name = name
        self.t = t
        self.w = None
        self.rs = {}
        self.dg = dg
        self.tw = 0.0
        self.tr = 0.0

    def __getitem__(self, idx):
        return self.t[idx]


class Sched:
    def __init__(self, nc, ctx):
        self.nc = nc
        self.ctx = ctx
        self.seq = {e: 0 for e in ENGS}
        self.waited = {e: {} for e in ENGS}
        self.sems = {e: ctx.enter_context(nc.semaphore('clk_' + e)) for e in ENGS}
        self.groups = []
        self.free_groups = []
        self.capture = None
        self.efree = {e: 0.0 for e in ENGS}
        self.scope_tiles = {}
        self.uid = 0

    def sb(self, name, shape, dtype=F32, ctx=None):
        self.uid += 1
        t = (ctx or self.ctx).enter_context(self.nc.sbuf_tensor('%s_%d' % (name, self.uid), list(shape), dtype))
        tl = Tile(name, t)
        if ctx is not None:
            self.scope_tiles.setdefault(id(ctx), []).append(tl)
        return tl

    def release(self, ctx):
        for tl in self.scope_tiles.pop(id(ctx), []):
            if tl.dg is not None:
                self.free_groups.append(tl.dg)
                tl.dg = None

    def ps(self, name, shape, dtype=F32):
        t = self.ctx.enter_context(self.nc.psum_tensor(name, list(shape), dtype))
        return Tile(name, t)

    def group(self, name):
        if self.free_groups:
            return self.free_groups.pop()
        self.uid += 1
        g = DmaGroup(self.ctx.enter_context(self.nc.semaphore('dg_%s_%d' % (name, self.uid))))
        self.groups.append(g)
        return g

    def _need(self, eng, dep, waits, kind):
        if dep is None:
            return
        if dep[0] == 'e':
            _, e2, s2 = dep
            if e2 == eng and kind != 'raw':
                return
            if self.waited[eng].get(e2, 0) >= s2:
                return
            self.waited[eng][e2] = s2
            waits.append((self.sems[e2], s2))
        else:
            dg = dep[1]
            c = dg.cnt
            if self.waited[eng].get(id(dg), 0) >= c:
                return
            self.waited[eng][id(dg)] = c
            waits.append((dg.sem, c))

    def _deps(self, eng, R, W, dma=False):
        waits = []
        for t in R:
            self._need(eng, t.w, waits, 'raw')
        for t in W:
            if not (dma and t.w is not None and t.w[0] == 'd'):
                self._need(eng, t.w, waits, 'waw')
            for r in t.rs.values():
                self._need(eng, r, waits, 'war')
        return waits

    ECOST = {PE: 0.12, DVE: 0.45, ACT: 0.45, POOL: 0.6, SP: 0.1}

    def est_start(self, item):
        kind, eng, R, W = item[0], item[1], item[3], item[4]
        t = self.efree[eng]
        for tl in R:
            t = max(t, tl.tw + HOP)
        for tl in W:
            t = max(t, tl.tw + HOP, tl.tr + HOP)
        return t

    def _est_update(self, kind, eng, R, W, cost):
        t0 = self.est_start((kind, eng, None, R, W))
        if kind == 'dma':
            self.efree[eng] = t0 + 0.1
            t1 = t0 + 2.5
        else:
            t1 = t0 + (cost if cost is not None else self.ECOST[eng])
            self.efree[eng] = t1
        for tl in W:
            tl.tw = t1
            tl.tr = 0.0
        for tl in R:
            tl.tr = max(tl.tr, t1)

    def run_window(self, q, W_):
        q = list(q)
        while q:
            best = None
            seen_r, seen_w = set(), set()
            for pos, it in enumerate(q[:W_]):
                r_ = set(id(t_) for t_ in it[3])
                w_ = set(id(t_) for t_ in it[4])
                if not (w_ & (seen_r | seen_w)) and not (r_ & seen_w):
                    es = self.est_start(it) + 0.02 * pos
                    if best is None or es < best[0]:
                        best = (es, pos)
                seen_r |= r_
                seen_w |= w_
            self.emit(q.pop(best[1]))

    def emit(self, item):
        cap, self.capture = self.capture, None
        if item[0] == 'op':
            self.op(item[1], item[2], item[3], item[4], cost=item[5])
        else:
            self.dma(item[1], item[2][0], item[2][1], item[3], item[4], **item[5])
        self.capture = cap

    def op(self, eng, fn, R=(), W=(), cost=None):
        if self.capture is not None:
            self.capture.append(('op', eng, fn, tuple(R), tuple(W), cost))
            return
        self._est_update('op', eng, R, W, cost)
        waits = self._deps(eng, R, W)
        self.seq[eng] += 1
        h = getattr(self.nc, eng)
        for (sm, v) in waits:
            h.wait_ge(sm, v)
        fn(h).then_inc(self.sems[eng], 1)
        me = ('e', eng, self.seq[eng])
        for t in W:
            t.w = me
            t.rs = {}
        for t in R:
            if t not in W:
                t.rs[eng] = me

    def dma(self, eng, out, in_, R=(), W=(), **kw):
        if self.capture is not None:
            self.capture.append(('dma', eng, (out, in_), tuple(R), tuple(W), kw))
            return
        self._est_update('dma', eng, R, W, None)
        waits = self._deps(eng, R, W, dma=True)
        tl = (list(W) + list(R))[0]
        if tl.dg is None:
            tl.dg = self.group(tl.name)
        dg = tl.dg
        dg.cnt += 16
        h = getattr(self.nc, eng)
        for (sm, v) in waits:
            h.wait_ge(sm, v)
        h.dma_start(out=out, in_=in_, **kw).then_inc(dg.sem, 16)
        me = ('d', dg, dg.cnt)
        for t in W:
            t.w = me
            t.rs = {}
        for t in R:
            t.rs[id(dg)] = me

    def barrier(self):
        for e in ENGS:
            h = getattr(self.nc, e)
            for e2 in ENGS:
                if e2 != e and self.seq[e2] > self.waited[e].get(e2, 0):
                    self.waited[e][e2] = self.seq[e2]
                    h.wait_ge(self.sems[e2], self.seq[e2])
            for g in self.groups:
                if g.cnt > self.waited[e].get(id(g), 0):
                    self.waited[e][id(g)] = g.cnt
                    h.wait_ge(g.sem, g.cnt)


D = 1024
KC = 8
TP = 2048
NSQ = 16
T = TP + NSQ * 4
DEPTH = 2
NIN = 9232
DFF = 2816
NFC = 44
EPS = 1e-6
O_GQKV, O_GZ, O_GB, O_GA = 0, 1536, 2048, 2052
O_MQ, O_MK, O_MV, O_MO, O_MI, O_MF = 2056, 2568, 3080, 3592, 4104, 4108
O_HQ, O_HF, O_HI, O_HG, O_BR = 4112, 4624, 5136, 5648, 6160


class Grp:
    def __init__(self, tok0, ntok, c, nch, tt, off, samp, idx, s0=0):
        self.tok0, self.ntok, self.c, self.nch, self.tt, self.off, self.samp, self.idx = tok0, ntok, c, nch, tt, off, samp, idx
        self.s0 = s0


NWQ = 4
NPG = 4
GROUPS = [Grp(512 * i, 512, 64, 8, i, 0, False, i) for i in range(NPG)] + [Grp(TP + 32 * j, 32, 4, 8, 4, 32 * j, True, NPG + j, 8 * j) for j in range(2)]
TT_N = [512, 512, 512, 512, 64]

PP = {}
_o = 0
for _n, _w in (('lnmix', 16), ('lnffn', 16), ('lnfin', 8), ('gconvw', 96), ('gnorm', 8), ('mnorm', 8), ('hnorm', 8),
               ('hlb', 8), ('scpar', 4), ('fconvw', 2 * NFC * 3), ('fconvb', 2 * NFC), ('min', 128), ('nin', 128)):
    PP[_n] = _o
    _o += _w
NPP = _o
CC = {}
_o = 0
for _n, _w in (('ident', 128), ('sel', 512), ('sel3', 512),
               ('mTc4', 64), ('mTs4', 64), ('mS4', 64), ('id4', 64), ('rs256', 512), ('rs64', 64),
               ('mTc64', 512), ('mTs64', 512), ('mS64', 512), ('id64', 512)):
    CC[_n] = _o
    _o += _w
NCC = _o
NCC_A = CC['mTc64']


def build_consts():
    c = np.zeros((128, NCC), np.float32)
    c[:, CC['ident']:CC['ident'] + 128] = np.eye(128)
    sel = np.zeros((128, 4, 128), np.float32)
    for q in range(4):
        for j in range(4):
            sel[32 * q + j, j, :] = 1.0
    c[:, CC['sel']:CC['sel'] + 512] = sel.reshape(128, 512)
    sel3 = np.zeros((128, 4, 128), np.float32)
    for j in range(4):
        sel3[96 + j, j, :] = 1.0
    c[:, CC['sel3']:CC['sel3'] + 512] = sel3.reshape(128, 512)
    for cs, nch, sfx in ((64, 8, '64'), (4, 16, '4')):
        s = np.arange(cs)[:, None]
        t = np.arange(cs)[None, :]
        for nm, m in (('mTc', t >= s), ('mTs', t > s), ('mS', t < s), ('id', t == s)):
            c[:cs, CC[nm + sfx]:CC[nm + sfx] + nch * cs] = np.tile(m.astype(np.float32), (1, nch))
    c[:, CC['rs256']:CC['rs256'] + 512] = (np.arange(512) % 64 != 0).astype(np.float32)[None, :]
    c[:, CC['rs64']:CC['rs64'] + 64] = (np.arange(64) % 4 != 0).astype(np.float32)[None, :]
    return c


CFG = dict(layers=DEPTH, mixers=(0, 1, 2), heads=4, merge=True, ffn=True, groups=None)


def build_nc():
    nc = bass.Bass("TRN2", target_bir_lowering=False)
    GRPS = GROUPS if CFG['groups'] is None else [GROUPS[i] for i in CFG['groups']]

    def din(name, shape):
        return nc.dram_tensor(name, list(shape), F32, kind="ExternalInput").ap()

    def dout(name, shape):
        return nc.dram_tensor(name, list(shape), F32, kind="ExternalOutput").ap()

    xT_d = din("xT", [D, T])
    w_in_d = din("w_in", [DEPTH, D, NIN])
    w_br_d = din("w_br", [DEPTH, 3, 512, D])
    w_out_d = din("w_out", [DEPTH, D, D])
    w_up_d = din("w_up", [DEPTH, D, 2 * DFF])
    w_dn_d = din("w_down", [DEPTH, DFF, D])
    pp_d = din("pp", [128, NPP])
    cc_d = din("cc", [128, NCC])
    sS_d = {m: din("s%s_S" % m, [DEPTH, NSQ, 4, 128, 128]) for m in 'gmh'}
    gcs_d = din("gcs", [DEPTH, NSQ * 3, 1536])
    fcs_d = din("fcs", [DEPTH, NSQ * 2, 2 * DFF])
    yT_d = dout("yT", [D, T])
    pS_o = {m: dout("p%s_S" % m, [DEPTH, 4, 128, 128]) for m in 'gmh'}
    sS_o = {m: dout("o%s_S" % m, [DEPTH, NSQ, 4, 128, 128]) for m in 'gmh'}
    pgc_o = dout("pgc", [DEPTH, 3, 1536])
    sgc_o = dout("sgc", [DEPTH, NSQ * 3, 1536])
    pfc_o = dout("pfc", [DEPTH, 2, 2 * DFF])
    sfc_o = dout("sfc", [DEPTH, NSQ * 2, 2 * DFF])
    nst_o = dout("nst", [128, 8 + 128])
    mst_o = dout("mst", [1, 8 + 128])

    with ExitStack() as ctx:
        S = Sched(nc, ctx)

        def fsz(ap):
            n_ = 1
            for d_ in ap.shape[1:]:
                n_ *= d_
            return n_

        def ecost(eng, ap):
            k = fsz(ap)
            if eng == DVE:
                return 0.08 + k * 1.05e-3
            if eng == ACT:
                return 0.2 + k * 0.75e-3
            if eng == POOL:
                return 0.3 + k * 1.5e-3
            return 0.05 + k * 0.45e-3

        def vtt(out, in0, in1, op, R, W, eng=DVE):
            S.op(eng, lambda e: e.tensor_tensor(out=out, in0=in0, in1=in1, op=op), R, W, cost=ecost(eng, out))

        def vts(out, in0, s1, s2, op0, op1, R, W, eng=DVE):
            if op1 is None:
                S.op(eng, lambda e: e.tensor_scalar(out=out, in0=in0, scalar1=s1, scalar2=None, op0=op0), R, W, cost=ecost(eng, out))
            else:
                S.op(eng, lambda e: e.tensor_scalar(out=out, in0=in0, scalar1=s1, scalar2=s2, op0=op0, op1=op1), R, W, cost=ecost(eng, out))

        def vstt(out, in0, sc, in1, op0, op1, R, W, eng=DVE):
            S.op(eng, lambda e: e.scalar_tensor_tensor(out=out, in0=in0, scalar=sc, in1=in1, op0=op0, op1=op1), R, W, cost=ecost(eng, out))

        def vcopy(out, in_, R, W, eng=DVE):
            if eng == ACT:
                S.op(ACT, lambda e: e.copy(out=out, in_=in_), R, W, cost=ecost(ACT, out))
            else:
                S.op(eng, lambda e: e.tensor_copy(out=out, in_=in_), R, W, cost=ecost(eng, out))

        def vrecip(out, in_, R, W):
            S.op(DVE, lambda e: e.reciprocal(out=out, in_=in_), R, W)

        def vset(ap, val, W, eng=DVE):
            S.op(eng, lambda e: e.memset(ap, val), (), W)

        def act(out, in_, func, R, W, bias=None, scale=None):
            kw = {}
            if bias is not None:
                kw['bias'] = bias
            if scale is not None:
                kw['scale'] = scale
            S.op(ACT, lambda e: e.activation(out=out, in_=in_, func=func, **kw), R, W, cost=ecost(ACT, out))

        def mm(out, lhsT, rhs, start, stop, R, W):
            S.op(PE, lambda e: e.matmul(out, lhsT=lhsT, rhs=rhs, start=start, stop=stop), R, W, cost=ecost(PE, rhs))

        def tr(out, in_, idn, R, W):
            S.op(PE, lambda e: e.transpose(out, in_, idn), R, W, cost=0.12)

        def scan(out, d0, d1, init, op0, op1, R, W):
            S.op(DVE, lambda e: e.tensor_tensor_scan(out=out, data0=d0, data1=d1, initial=init, op0=op0, op1=op1), R, W, cost=0.08 + fsz(out) * 2.1e-3)

        ACT_COPY = ACT if CFG.get('actcopy', 1) else DVE
        ACT_E = ACT if CFG.get('acte', 1) else DVE
        POOL_E = POOL if CFG.get('pool', 1) else DVE

        def a_sigmoid(out, in_, R, Wt):
            act(out, in_, AF.Exp, R, [Wt], scale=-1.0)
            act(out, out, AF.Ln, [Wt], [Wt], bias=1.0)
            act(out, out, AF.Exp, [Wt], [Wt], scale=-1.0)

        def a_rsqrt(out, in_, scale, R, Wt):
            act(out, in_, AF.Ln, list(R) + [epst], [Wt], bias=epst[0:out.shape[0], 0:1], scale=scale)
            act(out, out, AF.Exp, [Wt], [Wt], scale=-0.5)

        def a_recip(out, in_, R, Wt):
            act(out, in_, AF.Ln, R, [Wt])
            act(out, out, AF.Exp, [Wt], [Wt], scale=-1.0)
        xT = [None] * 5
        xscr = nc.dram_tensor("xscr", [128, KC * T], F32, kind="Internal").ap()
        hT = [S.sb("hT%d" % i, [128, KC, TT_N[i]], BF16) for i in range(5)]
        ob = S.sb("ob", [128, 4, T], BF16)
        mscr = nc.dram_tensor("mscr", [128, 8 * T], BF16, kind="Internal").ap()
        ppt = S.sb("pp", [128, NPP])
        cct = S.sb("cc", [128, NCC_A])
        ccb = S.sb("ccb", [64, NCC - NCC_A], BF16)
        ones_bf = S.sb("ones", [128, 128], BF16)
        epst = S.sb("eps", [128, 1])
        psb = [S.ps("ps%d" % i, [128, 512]) for i in range(8)]
        psi = [0]

        def PS():
            psi[0] = (psi[0] + 1) % 8
            return psb[psi[0]]

        psa = [0]

        def PSA():
            psa[0] = (psa[0] + 1) % 3
            return psb[psa[0]]

        nst = S.sb("nst", [128, 136])
        mst = S.sb("mst", [1, 136])
        wq = [S.sb("wq%d" % i, [128, KC, 128], BF16) for i in range(NWQ)]
        wqi = [0]

        def ident(n):
            return cct[0:n, CC['ident']:CC['ident'] + n]

        def ppc(name, idx):
            o = PP[name] + idx
            return ppt[:, o:o + 1]

        S.dma(SP, ppt[:], pp_d, W=[ppt])
        S.dma(SP, cct[:], cc_d[:, 0:NCC_A], W=[cct])
        S.dma(POOL, ccb[:], cc_d[0:64, NCC_A:NCC], W=[ccb])
        vset(ones_bf[:], 1.0, [ones_bf])
        vset(epst[:], EPS, [epst])
        xctx = [None]

        def x_alloc():
            xctx[0] = ExitStack()
            for i in range(5):
                xT[i] = S.sb("xT%d" % i, [128, KC, TT_N[i]], F32, xctx[0])

        def x_free():
            S.barrier()
            S.release(xctx[0])
            xctx[0].close()

        def xs_view(i):
            o = sum(TT_N[:i])
            return xscr.rearrange("p (kc t) -> p kc t", kc=KC)[:, :, o:o + TT_N[i]]

        def loadw(src, ncols=128):
            wqi[0] = (wqi[0] + 1) % NWQ
            w = wq[wqi[0]]
            S.dma(POOL, w[:, :, 0:ncols], src.rearrange("(kc p) j -> p kc j", p=128), W=[w])
            return w

        def proj(w, g, out, ncols=128):
            pst = out
            co = 0
            if isinstance(w, tuple):
                w, co = w
            for kc in range(KC):
                mm(pst[0:ncols, 0:g.ntok], w[:, kc, co:co + ncols], hT[g.tt][:, kc, g.off:g.off + g.ntok],
                   kc == 0, kc == KC - 1, [w, hT[g.tt]], [pst])

        def rmsnorm(dst_tiles, gname, gidx0, dst_dtype_f32=False, store=None):
            with ExitStack() as c2:
                sq = [S.sb("sq%d" % i, [128, 512], BF16, c2) for i in range(2)]
                rs = S.sb("rs", [128, 512], F32, c2)
                yo = [S.sb("yo%d" % i, [128, 512], F32, c2) for i in range(2)] if store is not None else None
                for tt in range(5):
                    n = TT_N[tt]
                    p = PS()
                    for kc in range(KC):
                        s_ = sq[kc % 2]
                        act(s_[:, 0:n], xT[tt][:, kc, :], AF.Square, [xT[tt]], [s_])
                        mm(p[:, 0:n], ones_bf[:], s_[:, 0:n], kc == 0, kc == KC - 1, [ones_bf, s_], [p])
                    a_rsqrt(rs[:, 0:n], p[:, 0:n], 1.0 / D, [p], rs)
                    for kc in range(KC):
                        if store is None:
                            vstt(dst_tiles[tt][:, kc, :], xT[tt][:, kc, :], ppc(gname, gidx0 + kc), rs[:, 0:n],
                                 ALU.mult, ALU.mult, [xT[tt], ppt, rs], [dst_tiles[tt]])
                        else:
                            y_ = yo[kc % 2]
                            vstt(y_[:, 0:n], xT[tt][:, kc, :], ppc(gname, gidx0 + kc), rs[:, 0:n],
                                 ALU.mult, ALU.mult, [xT[tt], ppt, rs], [y_])
                            o = sum(TT_N[:tt])
                            S.dma(SP, store[kc * 128:(kc + 1) * 128, o:o + n], y_[:, 0:n], R=[y_])
                S.barrier()
                S.release(c2)

        def cmask(name, g):
            sfx = '64' if g.c == 64 else '4'
            o = CC[name + sfx]
            if g.c == 64:
                return ccb[0:64, o - NCC_A:o - NCC_A + g.nch * g.c]
            return cct[0:g.c, o:o + g.nch * g.c]

        def mtile(g):
            return ccb if g.c == 64 else cct

        marks = []
        _NC_CACHE['marks'] = marks

        def mark(label):
            marks.append((label, dict(S.seq)))

        x_alloc()
        for i in range(5):
            o = sum(TT_N[:i])
            S.dma(SP, xT[i][:], xT_d[:, o:o + TT_N[i]].rearrange("(kc p) t -> p kc t", p=128), W=[xT[i]])
        for l in range(CFG['layers']):
            mark('L%d rmsnorm1' % l)
            if l > 0:
                for i in range(5):
                    S.dma(SP, xs_view(i), xT[i][:], R=[xT[i]])
            rmsnorm(hT, 'lnmix', l * 8)
            x_free()

            for bi in range(3):
                if bi not in CFG['mixers']:
                    continue
                mark('L%d mixer%d' % (l, bi))
                with ExitStack() as cm:
                    NT = 512
                    def TF(name, shape=None, dt=F32):
                        return S.sb(name, shape or [128, NT], dt, cm)
                    scw = TF("scw", [128, KC, 128], BF16)
                    vset(scw[:], 0.0, [scw])
                    for q, o in enumerate((O_GB, O_GA, O_MI, O_MF)):
                        S.dma(POOL, scw[:, :, 32 * q:32 * q + 4],
                              w_in_d[l][:, o:o + 4].rearrange("(kc p) j -> p kc j", p=128), W=[scw])

                    hwp = [TF("hwp%d" % i, [128, KC, 256], BF16) for i in range(4)]

                    def make_ctx(cix):
                        gcis = [TF("gci%d" % i, [48, 128]) for i in range(2)]
                        csts = [TF("cst%d" % i, [48, 128]) for i in range(2)]
                        cio = [0]
                        psa = [0]

                        def PSA():
                            psa[0] = (psa[0] + 1) % 2
                            return psb[2 * cix + psa[0]]

                        psr = [0]

                        def PS():
                            psr[0] = (psr[0] + 1) % 2
                            return psb[4 + 2 * cix + psr[0]]

                        f = [TF("f%d" % i) for i in range(10)]
                        b = [TF("b%d" % i, None, BF16) for i in range(8)]
                        tk = [TF("tk%d" % i, [64, 8 * 130], BF16) for i in range(3)]
                        sct = f[9]
                        uvt = TF("uvt", [64, 1024], BF16)
                        pre = TF("pre", [128, 3 + NT])
                        pre2 = TF("pre2", [128, 3 + NT])
                        cso2 = TF("cso2", [128, 48])
                        hist = [TF("hist%d" % i, [128, 3]) for i in range(3)]
                        cso = TF("cso", [128, 48])
                        Sst = [TF("Sst%d" % i, [128, 130]) for i in range(8)]
                        Sbf = [TF("Sbf%d" % i, [128, 130], BF16) for i in range(8)]
                        nrep = [TF("nrep%d" % i, [128, 128], BF16) for i in range(8)]
                        ubig = TF("ubig", [64, 8 * 128], BF16)
                        sm = [TF("sm%d" % i, [128, 16]) for i in range(6)]
                        mcar = TF("mcar", [128, 1])
                        cols = TF("cols", [64, 16])
                        prm = TF("prm", [128, 4])

                        act(prm[:, 0:1], ppc('scpar', l * 2 + 1), AF.Exp, [ppt], [prm])
                        vts(prm[:, 0:1], prm[:, 0:1], -1.0, None, ALU.mult, None, [prm], [prm])
                        vts(prm[:, 1:2], ppc('scpar', l * 2), -1.0, None, ALU.mult, None, [ppt], [prm])
                        vset(sct[:], 0.0, [sct])

                        def load_head_w(cols_list):
                            return [(hwp[i], cix * 128) for i in range(4)]

                        def scalars(g):
                            n = g.ntok
                            p = PS()
                            proj(scw, g, p)
                            a_sigmoid(sct[0:4, 0:n], p[0:4, 0:n], [p], sct)
                            act(sct[32:36, 0:n], p[32:36, 0:n], AF.Exp, [p, ppt], [sct], bias=ppt[32:36, PP['scpar'] + l * 2:PP['scpar'] + l * 2 + 1])
                            act(sct[32:36, 0:n], sct[32:36, 0:n], AF.Ln, [sct], [sct], bias=1.0)
                            vts(sct[32:36, 0:n], sct[32:36, 0:n], prm[32:36, 0:1], None, ALU.mult, None, [sct, prm], [sct])
                            act(sct[64:68, 0:n], p[64:68, 0:n], AF.Identity, [p, ppt], [sct], bias=ppt[64:68, PP['scpar'] + l * 2:PP['scpar'] + l * 2 + 1])
                            act(sct[96:100, 0:n], p[96:100, 0:n], AF.Exp, [p, prm], [sct], bias=prm[96:100, 1:2], scale=-1.0)
                            act(sct[96:100, 0:n], sct[96:100, 0:n], AF.Ln, [sct], [sct], bias=1.0)
                            vts(sct[96:100, 0:n], sct[96:100, 0:n], -1.0, None, ALU.mult, None, [sct], [sct])

                        def bcast(q, h, g, dst):
                            n = g.ntok
                            p = PS()
                            if q < 3:
                                so = CC['sel'] + h * 128
                                mm(p[:, 0:n], cct[32 * q:32 * q + 4, so:so + 128], sct[32 * q:32 * q + 4, 0:n], True, True, [cct, sct], [p])
                            else:
                                so = CC['sel3'] + h * 128
                                mm(p[:, 0:n], cct[64:100, so:so + 128], sct[64:100, 0:n], True, True, [cct, sct], [p])
                            vcopy(dst[:, 0:n], p[:, 0:n], [p], [dst], eng=ACT_COPY)

                        def to_tok(src, g, dst, width=128, stride=None, bank=None):
                            stride = stride or width
                            c, nch = g.c, g.nch
                            for b0 in range(0, nch, 4):
                                p = bank if bank is not None else PS()
                                for j in range(4):
                                    ci = b0 + j
                                    tr(p[0:c, j * 128:(j + 1) * 128], src[:, ci * c:(ci + 1) * c], ident(128), [src, cct], [p])
                                dv = dst[0:c, b0 * stride:(b0 + 4) * stride].rearrange("p (a w) -> p a w", a=4)[:, :, 0:128]
                                vcopy(dv, p[0:c, 0:512].rearrange("p (a w) -> p a w", a=4), [p], [dst], eng=ACT_E)

                        def post(osrc, osrc_t, gate_p, gate_f, normname, h, g, bi):
                            n = g.ntok
                            act(b[7][:, 0:n], osrc, AF.Square, [osrc_t], [b[7]])
                            p = PS()
                            mm(p[:, 0:n], ones_bf[:], b[7][:, 0:n], True, True, [ones_bf, b[7]], [p])
                            a_rsqrt(f[8][:, 0:n], p[:, 0:n], 1.0 / 128, [p], f[8])
                            vstt(f[8][:, 0:n], osrc, ppc(normname, l * 4 + h), f[8][:, 0:n], ALU.mult, ALU.mult, [osrc_t, ppt, f[8]], [f[8]])
                            if gate_f == AF.Silu:
                                vtt(f[8][:, 0:n], f[8][:, 0:n], gate_p[:, 0:n], ALU.mult, [f[8], gate_p], [f[8]])
                            a_sigmoid(f[7][:, 0:n], gate_p[:, 0:n], [gate_p], f[7])
                            vtt(ob[:, h, g.tok0:g.tok0 + n], f[8][:, 0:n], f[7][:, 0:n], ALU.mult, [f[8], f[7]], [ob])

                        def lastbc(t_, g):
                            c, nch = g.c, g.nch
                            v = t_[:, 0:g.ntok].rearrange("p (a c) -> p a c", c=c)
                            return v[:, :, c - 1:c].broadcast_to([128, nch, c]), v

                        def state_io(mname, h, g, ci, k, width, when):
                            if when == 'load':
                                S.dma(SP, Sst[k][:, 0:128], sS_d[mname][l, g.s0 + ci, h], W=[Sst[k]])
                            elif g.samp:
                                S.dma(SP, sS_o[mname][l, g.s0 + ci, h], Sst[k][:, 0:128], R=[Sst[k]])
                            else:
                                S.dma(SP, pS_o[mname][l, h], Sst[k][:, 0:128], R=[Sst[k]])


                        def hgrn_head(h):
                            ws = load_head_w([O_HQ + h * 128, O_HF + h * 128, O_HI + h * 128, O_HG + h * 128])
                            if l == 0:
                                vset(prm[:, 2:3], 0.0, [prm])
                            else:
                                vtt(prm[:, 2:3], ppc('hlb', 4 + h), ppc('hlb', h), ALU.subtract, [ppt], [prm])
                                yield
                                a_sigmoid(prm[:, 2:3], prm[:, 2:3], [prm], prm)
                                yield
                            vts(prm[:, 3:4], prm[:, 2:3], -1.0, 1.0, ALU.mult, ALU.add, [prm], [prm])
                            vset(Sst[0][:], 0.0, [Sst[0]])
                            vset(Sbf[0][:], 0.0, [Sbf[0]])
                            for g in GRPS:
                                n, c, nch = g.ntok, g.c, g.nch
                                rsm = cct[:, CC['rs256']:CC['rs256'] + n] if not g.samp else cct[:, CC['rs64']:CC['rs64'] + n]
                                qs, fg, G, kk, t1, kw, iT = f[0], f[1], f[2], f[3], f[4], f[5], f[6]
                                qe, ke = b[0], b[1]
                                kw_tok, i_tok = tk[0], tk[1]
                                p = PS(); proj(ws[0], g, p)
                                yield
                                a_sigmoid(qs[:, 0:n], p[:, 0:n], [p], qs)
                                yield
                                vtt(qs[:, 0:n], qs[:, 0:n], p[:, 0:n], ALU.mult, [qs, p], [qs])
                                yield
                                p = PS(); proj(ws[1], g, p)
                                yield
                                a_sigmoid(fg[:, 0:n], p[:, 0:n], [p], fg)
                                yield
                                vts(fg[:, 0:n], fg[:, 0:n], prm[:, 3:4], prm[:, 2:3], ALU.mult, ALU.add, [fg, prm], [fg])
                                yield
                                act(t1[:, 0:n], fg[:, 0:n], AF.Ln, [fg], [t1])
                                yield
                                vts(kk[:, 0:n], fg[:, 0:n], -1.0, 1.0, ALU.mult, ALU.add, [fg], [kk])
                                yield
                                scan(G[:, 0:n], rsm, t1[:, 0:n], 0.0, ALU.mult, ALU.add, [cct, t1], [G])
                                yield
                                act(t1[:, 0:n], G[:, 0:n], AF.Exp, [G], [t1])
                                yield
                                vtt(qe[:, 0:n], qs[:, 0:n], t1[:, 0:n], ALU.mult, [qs, t1], [qe])
                                yield
                                act(t1[:, 0:n], G[:, 0:n], AF.Exp, [G], [t1], scale=-1.0)
                                yield
                                vtt(ke[:, 0:n], kk[:, 0:n], t1[:, 0:n], ALU.mult, [kk, t1], [ke])
                                yield
                                lb_, Gv = lastbc(G, g)
                                vtt(t1[:, 0:n].rearrange("p (a c) -> p a c", c=c), lb_, Gv, ALU.subtract, [G], [t1])
                                yield
                                act(t1[:, 0:n], t1[:, 0:n], AF.Exp, [t1], [t1])
                                yield
                                vtt(kw[:, 0:n], kk[:, 0:n], t1[:, 0:n], ALU.mult, [kk, t1], [kw])
                                yield
                                act(t1[:, 0:n], G[:, 0:n], AF.Exp, [G], [t1])
                                yield
                                to_tok(kw, g, kw_tok)
                                yield
                                p = PS(); proj(ws[2], g, p)
                                yield
                                vcopy(iT[:, 0:n], p[:, 0:n], [p], [iT], eng=ACT_E)
                                yield
                                to_tok(iT, g, i_tok)
                                yield
                                pg = PSA(); proj(ws[3], g, pg)
                                vcopy(f[9][:, 0:n], pg[:, 0:n], [pg], [f[9]], eng=ACT_E)
                                pg = f[9]
                                yield
                                pp_ = PS()
                                for ci in range(nch):
                                    mm(pp_[0:c, ci * c:(ci + 1) * c], ke[:, ci * c:(ci + 1) * c], qe[:, ci * c:(ci + 1) * c], True, True, [ke, qe], [pp_])
                                PTs = b[2]
                                vtt(PTs[0:c, 0:n], pp_[0:c, 0:n], cmask('mTc', g), ALU.mult, [pp_, mtile(g)], [PTs])
                                yield
                                po = PSA()
                                if g.samp:
                                    for ci in range(nch):
                                        state_io('h', h, g, ci, ci, 128, 'load')
                                    yield
                                    for ci in range(nch):
                                        vcopy(Sbf[ci][:, 0:128], Sst[ci][:, 0:128], [Sst[ci]], [Sbf[ci]], eng=ACT_COPY)
                                    yield
                                    for ci in range(nch):
                                        cs = slice(ci * c, (ci + 1) * c)
                                        mm(po[:, cs], Sbf[ci][:, 0:128], qe[:, cs], True, False, [Sbf[ci], qe], [po])
                                        mm(po[:, cs], i_tok[0:c, ci * 128:(ci + 1) * 128], PTs[0:c, cs], False, True, [i_tok, PTs], [po])
                                    for b0 in range(0, nch, 4):
                                        pS_ = PS()
                                        for j in range(4):
                                            ci = b0 + j
                                            mm(pS_[:, j * 128:(j + 1) * 128], kw_tok[0:c, ci * 128:(ci + 1) * 128], i_tok[0:c, ci * 128:(ci + 1) * 128], True, True, [kw_tok, i_tok], [pS_])
                                        for j in range(4):
                                            ci = b0 + j
                                            ce = (ci + 1) * c - 1
                                            vstt(Sst[ci][:, 0:128], Sst[ci][:, 0:128], t1[:, ce:ce + 1], pS_[:, j * 128:(j + 1) * 128], ALU.mult, ALU.add, [Sst[ci], t1, pS_], [Sst[ci]])
                                        yield
                                    for ci in range(nch):
                                        state_io('h', h, g, ci, ci, 128, 'store')
                                    yield
                                for ci in (range(nch) if not g.samp else []):
                                    k = 0
                                    cs = slice(ci * c, (ci + 1) * c)
                                    mm(po[:, cs], Sbf[k][:, 0:128], qe[:, cs], True, False, [Sbf[k], qe], [po])
                                    mm(po[:, cs], i_tok[0:c, ci * 128:(ci + 1) * 128], PTs[0:c, cs], False, True, [i_tok, PTs], [po])
                                    pS_ = PS()
                                    mm(pS_[:, 0:128], kw_tok[0:c, ci * 128:(ci + 1) * 128], i_tok[0:c, ci * 128:(ci + 1) * 128], True, True, [kw_tok, i_tok], [pS_])
                                    ce = (ci + 1) * c - 1
                                    vstt(Sst[k][:, 0:128], Sst[k][:, 0:128], t1[:, ce:ce + 1], pS_[:, 0:128], ALU.mult, ALU.add, [Sst[k], t1, pS_], [Sst[k]])
                                    yield
                                    if g.samp or (g.idx == NPG - 1 and ci == nch - 1):
                                        state_io('h', h, g, ci, k, 128, 'store')
                                        yield
                                    if not g.samp:
                                        vcopy(Sbf[k][:, 0:128], Sst[k][:, 0:128], [Sst[k]], [Sbf[k]], eng=ACT_COPY)
                                        yield
                                post(po[:, 0:n], po, pg, AF.Silu, 'hnorm', h, g, 2)
                                yield

                        def mlstm_head(h):
                            ws = load_head_w([O_MQ + h * 128, O_MK + h * 128, O_MV + h * 128, O_MO + h * 128])
                            vset(Sst[0][:], 0.0, [Sst[0]])
                            vset(Sbf[0][:], 0.0, [Sbf[0]])
                            vset(nrep[0][:], 0.0, [nrep[0]])
                            vset(mcar[:], 0.0, [mcar])
                            v_tok, kw_tok = tk[0], tk[1]
                            for g in GRPS:
                                n, c, nch = g.ntok, g.c, g.nch
                                rsm = cct[:, CC['rs256']:CC['rs256'] + n] if not g.samp else cct[:, CC['rs64']:CC['rs64'] + n]
                                ig, lf, Fc, m, t1, t2, vT, t3 = f[0], f[1], f[2], f[3], f[4], f[5], f[6], f[7]
                                scalars(g)
                                yield
                                bcast(2, h, g, ig)
                                yield
                                bcast(3, h, g, lf)
                                yield
                                scan(Fc[:, 0:n], rsm, lf[:, 0:n], 0.0, ALU.mult, ALU.add, [cct, lf], [Fc])
                                yield
                                mp = sm[0]
                                mv = m[:, 0:n].rearrange("p (a c) -> p a c", c=c)
                                if not g.samp:
                                    vcopy(mp[:, 0:1], mcar[:, 0:1], [mcar], [mp])
                                    yield
                                    scan(m[:, 0:n], lf[:, 0:n], ig[:, 0:n], mcar[:, 0:1], ALU.add, ALU.max, [lf, ig, mcar], [m])
                                    yield
                                    vcopy(mp[:, 1:nch], mv[:, 0:nch - 1, c - 1], [m], [mp])
                                    yield
                                    vcopy(mcar[:, 0:1], m[:, n - 1:n], [m], [mcar])
                                    yield
                                else:
                                    mo = PP['min'] + l * 64 + g.s0 * 4 + h
                                    vcopy(mp[:, 0:nch], ppt[:, mo:mo + 4 * nch].rearrange("p (s h) -> p s h", h=4)[:, :, 0], [ppt], [mp])
                                    yield
                                    lfv = lf[:, 0:n].rearrange("p (a c) -> p a c", c=c)
                                    igv = ig[:, 0:n].rearrange("p (a c) -> p a c", c=c)
                                    for t_ in range(4):
                                        prev = mp[:, 0:nch] if t_ == 0 else mv[:, :, t_ - 1]
                                        vtt(sm[1][:, 0:nch], lfv[:, :, t_], prev, ALU.add, [lf, mp, m], [sm[1]])
                                        yield
                                        vtt(mv[:, :, t_], sm[1][:, 0:nch], igv[:, :, t_], ALU.max, [sm[1], ig], [m])
                                        yield
                                Fv = Fc[:, 0:n].rearrange("p (a c) -> p a c", c=c)
                                t1v = t1[:, 0:n].rearrange("p (a c) -> p a c", c=c)
                                inter = t2
                                vtt(t1v, Fv, mp[:, 0:nch].unsqueeze(2).broadcast_to([128, nch, c]), ALU.add, [Fc, mp], [t1])
                                yield
                                vtt(t1[:, 0:n], t1[:, 0:n], m[:, 0:n], ALU.subtract, [t1, m], [t1])
                                yield
                                act(inter[:, 0:n], t1[:, 0:n], AF.Exp, [t1], [inter])
                                yield
                                vtt(t1[:, 0:n], Fc[:, 0:n], m[:, 0:n], ALU.subtract, [Fc, m], [t1])
                                yield
                                act(t1[:, 0:n], t1[:, 0:n], AF.Exp, [t1], [t1])
                                yield
                                q2, qi, k2 = b[0], b[1], b[2]
                                p = PS(); proj(ws[0], g, p)
                                yield
                                vtt(q2[:, 0:n], p[:, 0:n], t1[:, 0:n], ALU.mult, [p, t1], [q2])
                                yield
                                vtt(qi[:, 0:n], p[:, 0:n], inter[:, 0:n], ALU.mult, [p, inter], [qi])
                                yield
                                vtt(t1[:, 0:n], ig[:, 0:n], Fc[:, 0:n], ALU.subtract, [ig, Fc], [t1])
                                yield
                                act(t3[:, 0:n], t1[:, 0:n], AF.Exp, [t1], [t3])
                                yield
                                Fl, _ = lastbc(Fc, g)
                                ml, _ = lastbc(m, g)
                                vtt(t1v, t1v, Fl, ALU.add, [t1, Fc], [t1])
                                yield
                                vtt(t1v, t1v, ml, ALU.subtract, [t1, m], [t1])
                                yield
                                act(t1[:, 0:n], t1[:, 0:n], AF.Exp, [t1], [t1])
                                yield
                                p = PS(); proj(ws[1], g, p)
                                yield
                                vstt(k2[:, 0:n], p[:, 0:n], 128.0 ** -0.5, t3[:, 0:n], ALU.mult, ALU.mult, [p, t3], [k2])
                                yield
                                vstt(t3[:, 0:n], p[:, 0:n], 128.0 ** -0.5, t1[:, 0:n], ALU.mult, ALU.mult, [p, t1], [t3])
                                yield
                                to_tok(t3, g, kw_tok)
                                yield
                                p = PS(); proj(ws[2], g, p)
                                yield
                                vcopy(vT[:, 0:n], p[:, 0:n], [p], [vT], eng=ACT_E)
                                yield
                                vset(v_tok[0:c, :], 1.0, [v_tok])
                                to_tok(vT, g, v_tok, 128, 130)
                                yield
                                pg = PS(); proj(ws[3], g, pg)
                                yield
                                vcopy(f[9][:, 0:n], pg[:, 0:n], [pg], [f[9]], eng=ACT_E)
                                yield
                                pg = f[9]
                                pp_ = PS()
                                for ci in range(nch):
                                    mm(pp_[0:c, ci * c:(ci + 1) * c], k2[:, ci * c:(ci + 1) * c], q2[:, ci * c:(ci + 1) * c], True, True, [k2, q2], [pp_])
                                PTs = b[3]
                                vtt(PTs[0:c, 0:n], pp_[0:c, 0:n], cmask('mTc', g), ALU.mult, [pp_, mtile(g)], [PTs])
                                yield
                                pnum = PSA()
                                pden = PSA()
                                if g.samp:
                                    for ci in range(nch):
                                        state_io('m', h, g, ci, ci, 128, 'load')
                                        no = PP['nin'] + l * 64 + (g.s0 + ci) * 4 + h
                                        vcopy(Sst[ci][:, 128:129], ppt[:, no:no + 1], [ppt], [Sst[ci]])
                                    yield
                                    for ci in range(nch):
                                        vcopy(Sbf[ci][:, 0:129], Sst[ci][:, 0:129], [Sst[ci]], [Sbf[ci]], eng=ACT_COPY)
                                        vcopy(nrep[ci][:], Sst[ci][:, 128:129].broadcast_to([128, 128]), [Sst[ci]], [nrep[ci]])
                                    yield
                                    for ci in range(nch):
                                        cs = slice(ci * c, (ci + 1) * c)
                                        mm(pnum[:, cs], Sbf[ci][:, 0:128], qi[:, cs], True, False, [Sbf[ci], qi], [pnum])
                                        mm(pnum[:, cs], v_tok[0:c, ci * 130:ci * 130 + 128], PTs[0:c, cs], False, True, [v_tok, PTs], [pnum])
                                        mm(pden[:, cs], nrep[ci][:], qi[:, cs], True, False, [nrep[ci], qi], [pden])
                                        mm(pden[:, cs], ones_bf[0:c, :], PTs[0:c, cs], False, True, [ones_bf, PTs], [pden])
                                    for b0 in range(0, nch, 3):
                                        pC = PS()
                                        nb = min(3, nch - b0)
                                        for j in range(nb):
                                            ci = b0 + j
                                            mm(pC[:, j * 129:(j + 1) * 129], kw_tok[0:c, ci * 128:(ci + 1) * 128], v_tok[0:c, ci * 130:ci * 130 + 129], True, True, [kw_tok, v_tok], [pC])
                                        for j in range(nb):
                                            ci = b0 + j
                                            ce = (ci + 1) * c - 1
                                            vstt(Sst[ci][:, 0:129], Sst[ci][:, 0:129], inter[:, ce:ce + 1], pC[:, j * 129:(j + 1) * 129], ALU.mult, ALU.add, [Sst[ci], inter, pC], [Sst[ci]])
                                        yield
                                    for ci in range(nch):
                                        ce = (ci + 1) * c - 1
                                        state_io('m', h, g, ci, ci, 128, 'store')
                                        so = 8 + l * 64 + (g.s0 + ci) * 4 + h
                                        vcopy(nst[:, so:so + 1], Sst[ci][:, 128:129], [Sst[ci]], [nst])
                                        vcopy(mst[0:1, so:so + 1], m[0:1, ce:ce + 1], [m], [mst])
                                    yield
                                for ci in (range(nch) if not g.samp else []):
                                    k = 0
                                    cs = slice(ci * c, (ci + 1) * c)
                                    mm(pnum[:, cs], Sbf[k][:, 0:128], qi[:, cs], True, False, [Sbf[k], qi], [pnum])
                                    mm(pnum[:, cs], v_tok[0:c, ci * 130:ci * 130 + 128], PTs[0:c, cs], False, True, [v_tok, PTs], [pnum])
                                    mm(pden[:, cs], nrep[k][:], qi[:, cs], True, False, [nrep[k], qi], [pden])
                                    mm(pden[:, cs], ones_bf[0:c, :], PTs[0:c, cs], False, True, [ones_bf, PTs], [pden])
                                    pC = PS()
                                    mm(pC[:, 0:129], kw_tok[0:c, ci * 128:(ci + 1) * 128], v_tok[0:c, ci * 130:ci * 130 + 129], True, True, [kw_tok, v_tok], [pC])
                                    ce = (ci + 1) * c - 1
                                    vstt(Sst[k][:, 0:129], Sst[k][:, 0:129], inter[:, ce:ce + 1], pC[:, 0:129], ALU.mult, ALU.add, [Sst[k], inter, pC], [Sst[k]])
                                    yield
                                    if g.samp or (g.idx == NPG - 1 and ci == nch - 1):
                                        state_io('m', h, g, ci, k, 128, 'store')
                                        yield
                                        so = (8 + l * 64 + (g.s0 + ci) * 4 + h) if g.samp else (l * 4 + h)
                                        vcopy(nst[:, so:so + 1], Sst[k][:, 128:129], [Sst[k]], [nst])
                                        yield
                                        vcopy(mst[0:1, so:so + 1], m[0:1, ce:ce + 1], [m], [mst])
                                        yield
                                    if not g.samp:
                                        vcopy(Sbf[k][:, 0:129], Sst[k][:, 0:129], [Sst[k]], [Sbf[k]], eng=ACT_COPY)
                                        yield
                                        vcopy(nrep[k][:], Sst[k][:, 128:129].broadcast_to([128, 128]), [Sst[k]], [nrep[k]])
                                        yield
                                vcopy(t1[:, 0:n], pden[:, 0:n], [pden], [t1])
                                yield
                                vstt(t1[:, 0:n], t1[:, 0:n], -1.0, t1[:, 0:n], ALU.mult, ALU.max, [t1], [t1])
                                yield
                                act(t3[:, 0:n], m[:, 0:n], AF.Exp, [m], [t3], scale=-1.0)
                                yield
                                vtt(t1[:, 0:n], t1[:, 0:n], t3[:, 0:n], ALU.max, [t1, t3], [t1])
                                yield
                                a_recip(t1[:, 0:n], t1[:, 0:n], [t1], t1)
                                yield
                                vtt(t3[:, 0:n], pnum[:, 0:n], t1[:, 0:n], ALU.mult, [pnum, t1], [t3])
                                yield
                                post(t3[:, 0:n], t3, pg, AF.Sigmoid, 'mnorm', h, g, 1)
                                yield

                        def gdn_head(h):
                            ws = load_head_w([O_GQKV + h * 128, O_GQKV + 512 + h * 128, O_GQKV + 1024 + h * 128, O_GZ + h * 128])
                            vset(Sst[0][:], 0.0, [Sst[0]])
                            vset(Sbf[0][:], 0.0, [Sbf[0]])
                            kw_tok, kbg_tok, bv_tok = tk[0], tk[1], tk[2]
                            for g in GRPS:
                                n, c, nch = g.ntok, g.c, g.nch
                                L = 6 if c == 64 else 2
                                if cix == 0 and h == 0 and l == 0:
                                    mark('  gdn g%d start' % g.idx)
                                rsm = cct[:, CC['rs256']:CC['rs256'] + n] if not g.samp else cct[:, CC['rs64']:CC['rs64'] + n]
                                beta, G, eG, wl, xs, r, kn, t1 = f[0], f[1], f[2], f[3], f[4], f[5], f[6], f[7]
                                qn, qg, knb, kb = b[0], b[1], b[2], b[3]
                                scalars(g)
                                yield
                                bcast(0, h, g, beta)
                                yield
                                bcast(1, h, g, t1)
                                yield
                                scan(G[:, 0:n], rsm, t1[:, 0:n], 0.0, ALU.mult, ALU.add, [cct, t1], [G])
                                yield
                                act(eG[:, 0:n], G[:, 0:n], AF.Exp, [G], [eG])
                                yield
                                lb_, Gv = lastbc(G, g)
                                vtt(wl[:, 0:n].rearrange("p (a c) -> p a c", c=c), lb_, Gv, ALU.subtract, [G], [wl])
                                yield
                                act(wl[:, 0:n], wl[:, 0:n], AF.Exp, [wl], [wl])
                                yield
                                if CFG.get('gstop', 99) <= 1:
                                    continue
                                def quantity(qi_, X, R):
                                    R_pre, R_xs, R_r, R_sq, R_bank, R_cso, R_gci, R_cst = R
                                    chunk = qi_ * 4 + h
                                    chunk = qi_ * 4 + h
                                    p = R_bank; proj(ws[qi_], g, p)
                                    yield
                                    if not g.samp:
                                        vcopy(R_pre[:, 3:3 + n], p[:, 0:n], [p], [R_pre], eng=ACT_E)
                                        yield
                                        if g.idx == 0:
                                            vset(R_pre[:, 0:3], 0.0, [R_pre])
                                        else:
                                            vcopy(R_pre[:, 0:3], hist[qi_][:, 0:3], [hist[qi_]], [R_pre], eng=POOL_E)
                                            yield
                                        vcopy(hist[qi_][:, 0:3], R_pre[:, n:n + 3], [R_pre], [hist[qi_]], eng=POOL_E)
                                        yield
                                        srcs = [R_pre[:, j:j + n] for j in range(4)]
                                        dst = R_xs[:, 0:n]
                                        if g.idx == NPG - 1:
                                            pt = R_bank
                                            tr(pt[0:3, 0:128], hist[qi_][:, 0:3], ident(128), [hist[qi_], cct], [pt])
                                            pass
                                            cst = R_cst
                                            vcopy(cst[0:3, :], pt[0:3, 0:128], [pt], [cst])
                                            yield
                                            S.dma(SP, pgc_o[l][:, chunk * 128:(chunk + 1) * 128], cst[0:3, :], R=[cst])
                                    else:
                                        pv = R_pre[:, 0:7 * nch].rearrange("p (s w) -> p s w", w=7)
                                        vcopy(pv[:, :, 3:7], p[:, 0:n].rearrange("p (s w) -> p s w", w=4), [p], [R_pre], eng=ACT_E)
                                        yield
                                        pt = R_bank
                                        pass
                                        gci = R_gci
                                        nr = 3 * nch
                                        S.dma(SP, gci[0:nr, :], gcs_d[l][g.s0 * 3:g.s0 * 3 + nr, chunk * 128:(chunk + 1) * 128], W=[gci])
                                        tr(pt[:, 0:nr], gci[0:nr, 0:128], ident(nr), [gci, cct], [pt])
                                        vcopy(pv[:, :, 0:3], pt[:, 0:nr].rearrange("p (s w) -> p s w", w=3), [pt], [R_pre])
                                        yield
                                        vcopy(R_cso[:, 0:nr].rearrange("p (s w) -> p s w", w=3), pv[:, :, 4:7], [R_pre], [R_cso])
                                        yield
                                        pt2 = R_bank
                                        tr(pt2[0:nr, 0:128], R_cso[:, 0:nr], ident(128), [R_cso, cct], [pt2])
                                        cst = R_cst
                                        vcopy(cst[0:nr, :], pt2[0:nr, 0:128], [pt2], [cst])
                                        yield
                                        S.dma(SP, sgc_o[l][g.s0 * 3:g.s0 * 3 + nr, chunk * 128:(chunk + 1) * 128], cst[0:nr, :], R=[cst])
                                        srcs = [pv[:, :, j:j + 4] for j in range(4)]
                                        dst = R_xs[:, 0:n].rearrange("p (s w) -> p s w", w=4)
                                    wo = PP['gconvw'] + (l * 12 + chunk) * 4
                                    if CFG.get('acttap', 0):
                                        psrc = p[:, 0:n] if not g.samp else p[:, 0:n].rearrange("p (s w) -> p s w", w=4)
                                        act(dst, psrc, AF.Identity, [p, ppt], [R_xs], scale=ppt[:, wo + 3:wo + 4])
                                        yield
                                        jr = range(0, 3)
                                    else:
                                        vts(dst, srcs[0], ppt[:, wo:wo + 1], None, ALU.mult, None, [R_pre, ppt], [R_xs])
                                        yield
                                        jr = range(1, 4)
                                    for j in jr:
                                        vstt(dst, srcs[j], ppt[:, wo + j:wo + j + 1], dst, ALU.mult, ALU.add, [R_pre, ppt, R_xs], [R_xs])
                                        yield
                                    a_sigmoid(R_r[:, 0:n], R_xs[:, 0:n], [R_xs], R_r)
                                    yield
                                    vtt(R_xs[:, 0:n], R_xs[:, 0:n], R_r[:, 0:n], ALU.mult, [R_xs, R_r], [R_xs])
                                    yield
                                    if X in 'qk':
                                        act(R_sq[:, 0:n], R_xs[:, 0:n], AF.Square, [R_xs], [R_sq])
                                        yield
                                        p2 = R_bank
                                        mm(p2[:, 0:n], ones_bf[:], R_sq[:, 0:n], True, True, [ones_bf, R_sq], [p2])
                                        a_rsqrt(R_r[:, 0:n], p2[:, 0:n], 1.0, [p2], R_r)
                                        yield
                                    if X == 'q':
                                        vstt(qn[:, 0:n], R_xs[:, 0:n], 128.0 ** -0.5, R_r[:, 0:n], ALU.mult, ALU.mult, [R_xs, R_r], [qn])
                                        yield
                                        vtt(qg[:, 0:n], qn[:, 0:n], eG[:, 0:n], ALU.mult, [qn, eG], [qg])
                                        yield
                                    elif X == 'k':
                                        vtt(kn[:, 0:n], R_xs[:, 0:n], R_r[:, 0:n], ALU.mult, [R_xs, R_r], [kn])
                                        yield
                                        vcopy(knb[:, 0:n], kn[:, 0:n], [kn], [knb])
                                        yield
                                        vtt(kb[:, 0:n], kn[:, 0:n], beta[:, 0:n], ALU.mult, [kn, beta], [kb])
                                        yield
                                        vtt(t1[:, 0:n], kn[:, 0:n], wl[:, 0:n], ALU.mult, [kn, wl], [t1])
                                        yield
                                        to_tok(t1, g, kw_tok, bank=R_bank)
                                        yield
                                        vtt(t1[:, 0:n], kn[:, 0:n], beta[:, 0:n], ALU.mult, [kn, beta], [t1])
                                        yield
                                        vtt(t1[:, 0:n], t1[:, 0:n], eG[:, 0:n], ALU.mult, [t1, eG], [t1])
                                        yield
                                        to_tok(t1, g, kbg_tok, bank=R_bank)
                                        yield
                                    else:
                                        vtt(R_xs[:, 0:n], R_xs[:, 0:n], beta[:, 0:n], ALU.mult, [R_xs, beta], [R_xs])
                                        yield
                                        to_tok(R_xs, g, bv_tok, bank=R_bank)
                                        yield
                                RES = [(pre, f[4], f[5], b[7], psb[4 + 2 * cix], cso, gcis[0], csts[0]),
                                       (pre2, f[8], f[9], b[6], psb[5 + 2 * cix], cso2, gcis[1], csts[1])]

                                def qv_chain():
                                    yield from quantity(0, 'q', RES[0])
                                    yield from quantity(2, 'v', RES[0])
                                fibs = [qv_chain(), quantity(1, 'k', RES[1])]
                                while fibs:
                                    for fb in list(fibs):
                                        try:
                                            next(fb)
                                        except StopIteration:
                                            fibs.remove(fb)
                                        yield
                                if CFG.get('gstop', 99) <= 2:
                                    continue
                                if cix == 0 and h == 0 and l == 0:
                                    mark('  gdn g%d qkv done' % g.idx)
                                pg = PSA(); proj(ws[3], g, pg)
                                vcopy(f[9][:, 0:n], pg[:, 0:n], [pg], [f[9]], eng=ACT_E)
                                pg = f[9]
                                yield
                                for b0 in range(0, nch, 4):
                                    p = PS()
                                    for j in range(4):
                                        ci = b0 + j
                                        tr(p[0:c, j * 128:(j + 1) * 128], G[:, ci * c:(ci + 1) * c], ident(128), [G, cct], [p])
                                    vcopy(cols[0:c, b0:b0 + 4], p[0:c, 0:512].rearrange("p (a w) -> p a w", a=4)[:, :, 0], [p], [cols])
                                    yield
                                DT, Dm = f[5], f[7]
                                for ci in range(nch):
                                    cs = slice(ci * c, (ci + 1) * c)
                                    vts(DT[0:c, cs], G[0:c, cs], cols[0:c, ci:ci + 1], 0.0, ALU.subtract, ALU.min, [G, cols], [DT])
                                    yield
                                    vts(Dm[0:c, cs], G[0:c, cs], -1.0, cols[0:c, ci:ci + 1], ALU.mult, ALU.add, [G, cols], [Dm])
                                    yield
                                vts(Dm[0:c, 0:n], Dm[0:c, 0:n], 0.0, None, ALU.min, None, [Dm], [Dm])
                                yield
                                act(DT[0:c, 0:n], DT[0:c, 0:n], AF.Exp, [DT], [DT])
                                yield
                                act(Dm[0:c, 0:n], Dm[0:c, 0:n], AF.Exp, [Dm], [Dm])
                                yield
                                vtt(Dm[0:c, 0:n], Dm[0:c, 0:n], cmask('mS', g), ALU.mult, [Dm, mtile(g)], [Dm], eng=POOL_E)
                                yield
                                DTs = f[6]
                                vtt(DTs[0:c, 0:n], DT[0:c, 0:n], cmask('mTs', g), ALU.mult, [DT, mtile(g)], [DTs], eng=POOL_E)
                                yield
                                vtt(DT[0:c, 0:n], DT[0:c, 0:n], cmask('mTc', g), ALU.mult, [DT, mtile(g)], [DT], eng=POOL_E)
                                yield
                                if CFG.get('gstop', 99) <= 3:
                                    continue
                                if cix == 0 and h == 0 and l == 0:
                                    mark('  gdn g%d decay done' % g.idx)
                                Xa, Ya, Ra = [b[4], b[5]], [b[6], b[7]], [tk[2], None]
                                Ra = [S_R[0], S_R[1]]
                                px = PS(); py = PS()
                                for ci in range(nch):
                                    cs = slice(ci * c, (ci + 1) * c)
                                    mm(py[0:c, cs], kb[:, cs], knb[:, cs], True, True, [kb, knb], [py])
                                    mm(px[0:c, cs], knb[:, cs], kb[:, cs], True, True, [kb, knb], [px])
                                vstt(Ya[0][0:c, 0:n], py[0:c, 0:n], -1.0, Dm[0:c, 0:n], ALU.mult, ALU.mult, [py, Dm], [Ya[0]])
                                yield
                                vstt(Xa[0][0:c, 0:n], px[0:c, 0:n], -1.0, DTs[0:c, 0:n], ALU.mult, ALU.mult, [px, DTs], [Xa[0]])
                                yield
                                vtt(Ra[0][0:c, 0:n], Xa[0][0:c, 0:n], cmask('id', g), ALU.add, [Xa[0], mtile(g)], [Ra[0]])
                                yield
                                cur = 0
                                if CFG.get('gstop', 99) <= 4:
                                    continue
                                for lev in range(1, min(L, CFG.get('glev', 99))):
                                    nx = 1 - cur
                                    px = PS(); py = PS()
                                    for ci in range(nch):
                                        cs = slice(ci * c, (ci + 1) * c)
                                        mm(px[0:c, cs], Ya[cur][0:c, cs], Xa[cur][0:c, cs], True, True, [Ya[cur], Xa[cur]], [px])
                                        mm(py[0:c, cs], Xa[cur][0:c, cs], Ya[cur][0:c, cs], True, True, [Ya[cur], Xa[cur]], [py])
                                    vcopy(Xa[nx][0:c, 0:n], px[0:c, 0:n], [px], [Xa[nx]])
                                    yield
                                    vcopy(Ya[nx][0:c, 0:n], py[0:c, 0:n], [py], [Ya[nx]], eng=ACT_COPY)
                                    yield
                                    pr = PS()
                                    for ci in range(nch):
                                        cs = slice(ci * c, (ci + 1) * c)
                                        mm(pr[0:c, cs], Ya[nx][0:c, cs], Ra[cur][0:c, cs], True, True, [Ya[nx], Ra[cur]], [pr])
                                    vtt(Ra[nx][0:c, 0:n], Ra[cur][0:c, 0:n], pr[0:c, 0:n], ALU.add, [Ra[cur], pr], [Ra[nx]])
                                    yield
                                    cur = nx
                                if CFG.get('gstop', 99) <= 5:
                                    continue
                                if cix == 0 and h == 0 and l == 0:
                                    mark('  gdn g%d inverse done' % g.idx)
                                TTm = Ra[cur]
                                nW = b[4] if cur == 1 else b[5]
                                nW = Xa[1 - cur]
                                pw = PS()
                                for ci in range(nch):
                                    cs = slice(ci * c, (ci + 1) * c)
                                    mm(pw[:, cs], kbg_tok[0:c, ci * 128:(ci + 1) * 128], TTm[0:c, cs], True, True, [kbg_tok, TTm], [pw])
                                vts(nW[:, 0:n], pw[:, 0:n], -1.0, None, ALU.mult, None, [pw], [nW])
                                yield
                                for b0 in range(0, nch, 4):
                                    puv = PS()
                                    for j in range(4):
                                        ci = b0 + j
                                        cs = slice(ci * c, (ci + 1) * c)
                                        mm(puv[0:c, j * 128:(j + 1) * 128], TTm[0:c, cs], bv_tok[0:c, ci * 128:(ci + 1) * 128], True, True, [TTm, bv_tok], [puv])
                                    vcopy(uvt[0:c, b0 * 128:(b0 + 4) * 128], puv[0:c, 0:512], [puv], [uvt], eng=ACT_E)
                                    yield
                                pq_ = PS()
                                for ci in range(nch):
                                    cs = slice(ci * c, (ci + 1) * c)
                                    mm(pq_[0:c, cs], knb[:, cs], qn[:, cs], True, True, [knb, qn], [pq_])
                                QTs = Ya[1 - cur]
                                vtt(QTs[0:c, 0:n], pq_[0:c, 0:n], DT[0:c, 0:n], ALU.mult, [pq_, DT], [QTs])
                                yield
                                if CFG.get('gstop', 99) <= 6:
                                    continue
                                if cix == 0 and h == 0 and l == 0:
                                    mark('  gdn g%d seq start' % g.idx)
                                po = PSA()
                                if g.samp:
                                    for ci in range(nch):
                                        state_io('g', h, g, ci, ci, 128, 'load')
                                    yield
                                    for ci in range(nch):
                                        vcopy(Sbf[ci][:, 0:128], Sst[ci][:, 0:128], [Sst[ci]], [Sbf[ci]], eng=ACT_COPY)
                                    yield
                                    for b0 in range(0, nch, 4):
                                        pu = PS()
                                        for j in range(4):
                                            ci = b0 + j
                                            cs = slice(ci * c, (ci + 1) * c)
                                            mm(pu[0:c, j * 128:(j + 1) * 128], nW[:, cs], Sbf[ci][:, 0:128], True, True, [nW, Sbf[ci]], [pu])
                                        vtt(ubig[0:c, b0 * 128:(b0 + 4) * 128], pu[0:c, 0:512], uvt[0:c, b0 * 128:(b0 + 4) * 128], ALU.add, [pu, uvt], [ubig])
                                        yield
                                    for ci in range(nch):
                                        cs = slice(ci * c, (ci + 1) * c)
                                        mm(po[:, cs], Sbf[ci][:, 0:128], qg[:, cs], True, False, [Sbf[ci], qg], [po])
                                        mm(po[:, cs], ubig[0:c, ci * 128:(ci + 1) * 128], QTs[0:c, cs], False, True, [ubig, QTs], [po])
                                    for b0 in range(0, nch, 4):
                                        pS_ = PS()
                                        for j in range(4):
                                            ci = b0 + j
                                            mm(pS_[:, j * 128:(j + 1) * 128], kw_tok[0:c, ci * 128:(ci + 1) * 128], ubig[0:c, ci * 128:(ci + 1) * 128], True, True, [kw_tok, ubig], [pS_])
                                        for j in range(4):
                                            ci = b0 + j
                                            ce = (ci + 1) * c - 1
                                            vstt(Sst[ci][:, 0:128], Sst[ci][:, 0:128], eG[:, ce:ce + 1], pS_[:, j * 128:(j + 1) * 128], ALU.mult, ALU.add, [Sst[ci], eG, pS_], [Sst[ci]])
                                        yield
                                    for ci in range(nch):
                                        state_io('g', h, g, ci, ci, 128, 'store')
                                    yield
                                for ci in (range(nch) if not g.samp else []):
                                    k = 0
                                    cs = slice(ci * c, (ci + 1) * c)
                                    pu = PS()
                                    if CFG.get('dbg', 0) == 1:
                                        mm(pu[0:c, 0:128], TTm[0:c, cs], bv_tok[0:c, ci * 128:(ci + 1) * 128], True, True, [TTm, bv_tok], [pu])
                                    else:
                                        mm(pu[0:c, 0:128], nW[:, cs], Sbf[k][:, 0:128], True, True, [nW, Sbf[k]], [pu])
                                    uo = (ci % 2) * 128
                                    vtt(ubig[0:c, uo:uo + 128], pu[0:c, 0:128], uvt[0:c, ci * 128:(ci + 1) * 128], ALU.add, [pu, uvt], [ubig])
                                    yield
                                    if CFG.get('dbg', 0) == 2:
                                        continue
                                    mm(po[:, cs], Sbf[k][:, 0:128], qg[:, cs], True, False, [Sbf[k], qg], [po])
                                    if CFG.get('dbg', 0) == 5:
                                        mm(po[:, cs], kw_tok[0:c, ci * 128:(ci + 1) * 128], QTs[0:c, cs], False, True, [kw_tok, QTs], [po])
                                    else:
                                        mm(po[:, cs], ubig[0:c, uo:uo + 128], QTs[0:c, cs], False, True, [ubig, QTs], [po])
                                    if CFG.get('dbg', 0) == 3:
                                        continue
                                    pS_ = PS()
                                    mm(pS_[:, 0:128], kw_tok[0:c, ci * 128:(ci + 1) * 128], ubig[0:c, uo:uo + 128], True, True, [kw_tok, ubig], [pS_])
                                    ce = (ci + 1) * c - 1
                                    vstt(Sst[k][:, 0:128], Sst[k][:, 0:128], eG[:, ce:ce + 1], pS_[:, 0:128], ALU.mult, ALU.add, [Sst[k], eG, pS_], [Sst[k]])
                                    yield
                                    if g.samp or (g.idx == NPG - 1 and ci == nch - 1):
                                        state_io('g', h, g, ci, k, 128, 'store')
                                        yield
                                    if not g.samp:
                                        vcopy(Sbf[k][:, 0:128], Sst[k][:, 0:128], [Sst[k]], [Sbf[k]], eng=ACT_COPY)
                                        yield
                                if CFG.get('dbg', 0) in (2, 3, 4):
                                    continue
                                if cix == 0 and h == 0 and l == 0:
                                    mark('  gdn g%d seq done' % g.idx)
                                post(po[:, 0:n], po, pg, AF.Silu, 'gnorm', h, g, 0)
                                yield

                        S_R = [TF("Ra%d" % i, [64, NT], BF16) for i in range(2)]
                        return (gdn_head, mlstm_head, hgrn_head)

                    ctxs = [make_ctx(i) for i in range(2)]


                    for h0 in range(0, CFG['heads'], 2):
                        cols_ = ([O_GQKV, O_GQKV + 512, O_GQKV + 1024, O_GZ], [O_MQ, O_MK, O_MV, O_MO], [O_HQ, O_HF, O_HI, O_HG])[bi]
                        for i_, o_ in enumerate(cols_):
                            o2 = o_ + h0 * 128
                            S.dma(POOL, hwp[i_][:], w_in_d[l][:, o2:o2 + 256].rearrange("(kc p) j -> p kc j", p=128), W=[hwp[i_]])
                        gens = [ctxs[j][bi](h0 + j) for j in range(2) if h0 + j < CFG['heads']]
                        if not CFG.get('lsched', 1):
                            while gens:
                                for gnr in list(gens):
                                    try:
                                        next(gnr)
                                    except StopIteration:
                                        gens.remove(gnr)
                            continue
                        qs = [[] for _ in gens]
                        rr = [0]
                        alive = [True] * len(gens)
                        while True:
                            for k_, gnr in enumerate(gens):
                                while alive[k_] and len(qs[k_]) < 256:
                                    S.capture = qs[k_]
                                    try:
                                        next(gnr)
                                    except StopIteration:
                                        alive[k_] = False
                                    S.capture = None
                            cands = [k_ for k_ in range(len(gens)) if qs[k_]]
                            if not cands:
                                break
                            if CFG.get('lsched', 1) == 2 or bi not in CFG.get('lsm', (0, 1, 2)):
                                rr[0] += 1
                                kb = cands[rr[0] % len(cands)]
                            else:
                                W_ = CFG.get('win', 96)
                                if W_ <= 1:
                                    kb = min(cands, key=lambda k_: S.est_start(qs[k_][0]))
                                    S.emit(qs[kb].pop(0))
                                    continue
                                best = None
                                for k_ in cands:
                                    seen_r, seen_w = set(), set()
                                    for pos, it in enumerate(qs[k_][:W_]):
                                        r_ = set(id(t_) for t_ in it[3])
                                        w_ = set(id(t_) for t_ in it[4])
                                        indep = not (w_ & (seen_r | seen_w)) and not (r_ & seen_w)
                                        if indep:
                                            es = S.est_start(it) + 0.02 * pos
                                            if best is None or es < best[0]:
                                                best = (es, k_, pos)
                                        seen_r |= r_
                                        seen_w |= w_
                                S.emit(qs[best[1]].pop(best[2]))
                                continue
                            S.emit(qs[kb].pop(0))
                    S.barrier()
                    S.release(cm)
                mark('L%d merge%d' % (l, bi))
                if not CFG['merge']:
                    continue
                if bi == 2:
                    x_alloc()
                    for i in range(5):
                        if l == 0:
                            o_ = sum(TT_N[:i])
                            S.dma(SP, xT[i][:], xT_d[:, o_:o_ + TT_N[i]].rearrange("(kc p) t -> p kc t", p=128), W=[xT[i]])
                        else:
                            S.dma(SP, xT[i][:], xs_view(i), W=[xT[i]])
                with ExitStack() as cg:
                    merged = S.sb("merged", [128, 8, T], BF16, cg)
                    mgt = [S.sb("mgt%d" % i, [128, 512], F32, cg) for i in range(2)]
                    mview = mscr.rearrange("p (k t) -> p k t", k=8)
                    if bi > 0:
                        for k2 in range(0, 8, 2):
                            S.dma(SP, merged[:, k2:k2 + 2, :], mview[:, k2:k2 + 2, :], W=[merged])
                    def merge(bi):
                        def ldm(dc):
                            wg_ = loadw(w_in_d[l][:, O_BR + bi * D + dc * 128:O_BR + bi * D + (dc + 1) * 128])
                            wqi[0] = (wqi[0] + 1) % NWQ
                            wb2 = wq[wqi[0]]
                            S.dma(POOL, wb2[:, 0:4, :], w_br_d[l, bi][:, dc * 128:(dc + 1) * 128].rearrange("(h p) j -> p h j", p=128), W=[wb2])
                            return wg_, wb2
                        nxt = ldm(0)
                        for dc in range(KC):
                            wg, wb_ = nxt
                            if dc + 1 < KC:
                                nxt = ldm(dc + 1)
                            for tt in range(5):
                                n = TT_N[tt]
                                o = sum(TT_N[:tt])
                                pgt = PS()
                                for kc in range(KC):
                                    mm(pgt[:, 0:n], wg[:, kc, :], hT[tt][:, kc, :], kc == 0, kc == KC - 1, [wg, hT[tt]], [pgt])
                                pb = PS()
                                for hh in range(4):
                                    mm(pb[:, 0:n], wb_[:, hh, :], ob[:, hh, o:o + n], hh == 0, hh == 3, [wb_, ob], [pb])
                                gt = mgt[tt % 2]
                                act(gt[:, 0:n], pgt[:, 0:n], AF.Sigmoid, [pgt], [gt])
                                if bi == 0:
                                    vtt(merged[:, dc, o:o + n], gt[:, 0:n], pb[:, 0:n], ALU.mult, [gt, pb], [merged])
                                else:
                                    vtt(gt[:, 0:n], gt[:, 0:n], pb[:, 0:n], ALU.mult, [gt, pb], [gt])
                                    vtt(merged[:, dc, o:o + n], merged[:, dc, o:o + n], gt[:, 0:n], ALU.add, [gt, merged], [merged])

                    mq = []
                    S.capture = mq
                    merge(bi)
                    if bi < 2:
                        for k2 in range(0, 8, 2):
                            S.dma(SP, mview[:, k2:k2 + 2, :], merged[:, k2:k2 + 2, :], R=[merged])
                    else:
                        mark('L%d wout' % l)
                        nxw = loadw(w_out_d[l][:, 0:128])
                        for dc in range(KC):
                            w = nxw
                            if dc + 1 < KC:
                                nxw = loadw(w_out_d[l][:, (dc + 1) * 128:(dc + 2) * 128])
                            for tt in range(5):
                                n = TT_N[tt]
                                o = sum(TT_N[:tt])
                                p = PS()
                                for kc in range(KC):
                                    mm(p[:, 0:n], w[:, kc, :], merged[:, kc, o:o + n], kc == 0, kc == KC - 1, [w, merged], [p])
                                vtt(xT[tt][:, dc, :], xT[tt][:, dc, :], p[:, 0:n], ALU.add, [xT[tt], p], [xT[tt]])
                    S.capture = None
                    S.run_window(mq, CFG.get('mwin', 96))
                    S.barrier()
                    S.release(cg)
            if not CFG['merge']:
                x_alloc()
                for i in range(5):
                    S.dma(SP, xT[i][:], xs_view(i), W=[xT[i]])

            mark('L%d rmsnorm2' % l)
            rmsnorm(hT, 'lnffn', l * 8)
            mark('L%d ffn' % l)
            with ExitStack() as cf:
                ua2 = [[S.sb("ua%d%d" % (i, k), [128, 2 + 512], F32, cf) for k in range(2)] for i in range(2)]
                cv2 = [[S.sb("cv%d%d" % (i, k), [128, 512], F32, cf) for k in range(2)] for i in range(2)]
                NJ = 6
                wd = S.sb("wd", [128, NJ, D], BF16, cf)
                fit = [0]
                fcis = [S.sb("fci%d" % i, [32, 128], F32, cf) for i in range(2)]
                fsts = [S.sb("fst%d" % i, [32, 128], F32, cf) for i in range(2)]
                fio = [0]
                fh = [[S.sb("fh%d%d" % (i, j), [128, 2], F32, cf) for j in range(2)] for i in range(2)]
                fcs2 = [S.sb("fcso%d" % i, [128, 32], F32, cf) for i in range(2)]
                gbuf = S.sb("gbuf", [128, NJ, T], BF16, cf)
                NPASS = 4 if CFG['ffn'] else 0
                JS = [0, 6, 12, 17, 22]
                def ldu(j):
                    return [loadw(w_up_d[l][:, (half * 22 + j) * 128:(half * 22 + j + 1) * 128]) for half in range(2)]
                ffq = []
                if CFG.get('ffwin', 128) > 1:
                    S.capture = ffq
                nxu = ldu(0) if NPASS else None
                pend = []
                for ps_ in range(NPASS):
                    j0 = JS[ps_]
                    nj = JS[ps_ + 1] - j0
                    S.dma(POOL, wd[:, 0:nj, :], w_dn_d[l][j0 * 128:(j0 + nj) * 128, :].rearrange("(j p) d -> p j d", p=128), W=[wd])
                    for jj in range(nj):
                        j = j0 + jj
                        wts = nxu
                        if j + 1 < 22:
                            nxu = ldu(j + 1)
                        dq = []
                        for half in range(2):
                            fc = half * 22 + j
                            S.dma(SP, fcis[half][:], fcs_d[l][:, fc * 128:(fc + 1) * 128], W=[fcis[half]])
                        for tt in range(5):
                            n = TT_N[tt]
                            o = sum(TT_N[:tt])
                            fit[0] += 1
                            ua = [ua2[0][fit[0] % 2], ua2[1][fit[0] % 2]]
                            cv = [cv2[0][fit[0] % 2], cv2[1][fit[0] % 2]]
                            for half in range(2):
                                fc = half * 22 + j
                                u_ = ua[half]
                                p = PS()
                                for kc in range(KC):
                                    mm(p[:, 0:n], wts[half][:, kc, :], hT[tt][:, kc, :], kc == 0, kc == KC - 1, [wts[half], hT[tt]], [p])
                                wo = PP['fconvw'] + (l * NFC + fc) * 3
                                bo = PP['fconvb'] + l * NFC + fc
                                if tt < 4:
                                    vcopy(u_[:, 2:2 + n], p[:, 0:n], [p], [u_], eng=ACT_COPY)
                                    if tt == 0:
                                        vset(u_[:, 0:2], 0.0, [u_])
                                    else:
                                        vcopy(u_[:, 0:2], fh[half][0][:, 0:2], [fh[half][0]], [u_], eng=POOL_E)
                                    vcopy(fh[half][0][:, 0:2], u_[:, n:n + 2], [u_], [fh[half][0]], eng=POOL_E)
                                    if tt == 3:
                                        vcopy(fh[half][1][:, 0:2], u_[:, n:n + 2], [u_], [fh[half][1]], eng=POOL_E)

                                        def _pst(half=half, fc=fc):
                                            pt = PS()
                                            S.op(PE, lambda e: e.transpose(pt[0:2, 0:128], fh[half][1][:, 0:2], ident(128)), [fh[half][1], cct], [pt])
                                            fst = fsts[half]
                                            vcopy(fst[0:2, :], pt[0:2, 0:128], [pt], [fst])
                                            S.dma(SP, pfc_o[l][:, fc * 128:(fc + 1) * 128], fst[0:2, :], R=[fst])
                                        dq.append(_pst)
                                    srcs = [u_[:, jx:jx + n] for jx in range(3)]
                                    dst = cv[half][:, 0:n]
                                else:
                                    uv = u_[:, 0:96].rearrange("p (s w) -> p s w", w=6)
                                    vcopy(uv[:, :, 2:6], p[:, 0:n].rearrange("p (s w) -> p s w", w=4), [p], [u_])
                                    pt = PS()
                                    fci = fcis[half]
                                    S.op(PE, (lambda pt=pt, fci=fci: (lambda e: e.transpose(pt[:, 0:32], fci[0:32, 0:128], ident(32))))(), [fci, cct], [pt])
                                    vcopy(uv[:, :, 0:2], pt[:, 0:32].rearrange("p (s w) -> p s w", w=2), [pt], [u_])
                                    fcsh = fcs2[half]
                                    vcopy(fcsh[:, 0:32].rearrange("p (s w) -> p s w", w=2), uv[:, :, 4:6], [u_], [fcsh], eng=POOL_E)

                                    def _sst(half=half, fc=fc, fcsh=fcsh):
                                        pt2 = PS()
                                        S.op(PE, lambda e: e.transpose(pt2[0:32, 0:128], fcsh[:, 0:32], ident(128)), [fcsh, cct], [pt2])
                                        fst = fsts[half]
                                        vcopy(fst[0:32, :], pt2[0:32, 0:128], [pt2], [fst])
                                        S.dma(SP, sfc_o[l][:, fc * 128:(fc + 1) * 128], fst[0:32, :], R=[fst])
                                    dq.append(_sst)
                                    srcs = [uv[:, :, jx:jx + 4] for jx in range(3)]
                                    dst = cv[half][:, 0:n].rearrange("p (s w) -> p s w", w=4)
                                if CFG.get('acttap', 0):
                                    psrc = p[:, 0:n] if tt < 4 else p[:, 0:n].rearrange("p (s w) -> p s w", w=4)
                                    act(dst, psrc, AF.Identity, [p, ppt], [cv[half]], scale=ppt[:, wo + 2:wo + 3], bias=ppt[:, bo:bo + 1])
                                    jr = range(0, 2)
                                else:
                                    vts(dst, srcs[0], ppt[:, wo:wo + 1], ppt[:, bo:bo + 1], ALU.mult, ALU.add, [u_, ppt], [cv[half]], eng=POOL_E)
                                    jr = range(1, 3)
                                for jx in jr:
                                    vstt(dst, srcs[jx], ppt[:, wo + jx:wo + jx + 1], dst, ALU.mult, ALU.add, [u_, ppt, cv[half]], [cv[half]])
                            act(cv[0][:, 0:n], cv[0][:, 0:n], AF.Silu, [cv[0]], [cv[0]])
                            vtt(gbuf[:, jj, o:o + n], cv[0][:, 0:n], cv[1][:, 0:n], ALU.mult, [cv[0], cv[1]], [gbuf])
                            if tt == 0:
                                for fn_ in pend:
                                    fn_()
                                pend = []
                        pend = dq
                    for dc in range(KC):
                        for tt in range(5):
                            n = TT_N[tt]
                            o = sum(TT_N[:tt])
                            p = PS()
                            for jj in range(nj):
                                mm(p[:, 0:n], wd[:, jj, dc * 128:(dc + 1) * 128], gbuf[:, jj, o:o + n], jj == 0, jj == nj - 1, [wd, gbuf], [p])
                            vtt(xT[tt][:, dc, :], xT[tt][:, dc, :], p[:, 0:n], ALU.add, [xT[tt], p], [xT[tt]])
                for fn_ in pend:
                    fn_()
                S.capture = None
                S.run_window(ffq, CFG.get('ffwin', 128))
                S.barrier()
                S.release(cf)

        mark('final')
        rmsnorm(None, 'lnfin', 0, store=yT_d)
        x_free()
        S.dma(SP, nst_o, nst[:], R=[nst])
        S.dma(SP, mst_o, mst[:], R=[mst])
        S.barrier()
        _NC_CACHE['stats'] = (dict(S.seq), len(S.groups), max(g.cnt for g in S.groups))
    return nc


_NC_CACHE = {}


def kernel(**inp):
    f32 = np.float32
    x_prompt = np.asarray(inp['x_prompt'], f32)
    x_sample = np.asarray(inp['x_sample'], f32)
    cc = build_consts()

    def part(a, nch):
        return np.ascontiguousarray(np.asarray(a, f32).reshape(nch, 128).T)

    pp0 = np.zeros((128, NPP), f32)
    for l in range(DEPTH):
        pp0[:, PP['lnmix'] + l * 8:PP['lnmix'] + (l + 1) * 8] = part(inp['ln_mix'][l], 8)
        pp0[:, PP['lnffn'] + l * 8:PP['lnffn'] + (l + 1) * 8] = part(inp['ln_ffn'][l], 8)
        gw = np.asarray(inp['gdn_conv_w'][l], f32)
        pp0[:, PP['gconvw'] + l * 48:PP['gconvw'] + (l + 1) * 48] = gw.T.reshape(12, 128, 4).transpose(1, 0, 2).reshape(128, 48)
        for nm, key in (('gnorm', 'gdn_norm'), ('mnorm', 'm_norm'), ('hnorm', 'hgrn_norm'), ('hlb', 'hgrn_lb')):
            pp0[:, PP[nm] + l * 4:PP[nm] + (l + 1) * 4] = part(inp[key][l], 4)
        pp0[32:36, PP['scpar'] + l * 2] = np.asarray(inp['gdn_dt_bias'][l], f32)
        pp0[64:68, PP['scpar'] + l * 2] = np.asarray(inp['m_ibias'][l], f32)
        pp0[96:100, PP['scpar'] + l * 2] = np.asarray(inp['m_fbias'][l], f32)
        pp0[32:36, PP['scpar'] + l * 2 + 1] = np.asarray(inp['gdn_A_log'][l], f32)
        fw = np.asarray(inp['ffn_conv_w'][l], f32)
        pp0[:, PP['fconvw'] + l * NFC * 3:PP['fconvw'] + (l + 1) * NFC * 3] = fw.T.reshape(NFC, 128, 3).transpose(1, 0, 2).reshape(128, NFC * 3)
        pp0[:, PP['fconvb'] + l * NFC:PP['fconvb'] + (l + 1) * NFC] = part(inp['ffn_conv_b'][l], NFC)
    pp0[:, PP['lnfin']:PP['lnfin'] + 8] = part(inp['ln_final'], 8)

    w_in = np.ascontiguousarray(inp['w_in'], f32)
    w_br = np.ascontiguousarray(inp['w_br'], f32)
    w_out = np.ascontiguousarray(inp['w_out'], f32)
    w_up = np.ascontiguousarray(inp['w_up'], f32)
    w_down = np.ascontiguousarray(inp['w_down'], f32)
    in_maps = []
    for c in range(8):
        sl = slice(c * NSQ, (c + 1) * NSQ)
        xs = np.concatenate([x_prompt[c], x_sample[sl].reshape(NSQ * 4, D)], axis=0)
        pp = pp0.copy()
        m_in = np.asarray(inp['state_mlstm_m'], f32)[:, sl]
        pp[:, PP['min']:PP['min'] + 128] = m_in.reshape(1, 128)
        n_in = np.asarray(inp['state_mlstm_n'], f32)[:, sl]
        pp[:, PP['nin']:PP['nin'] + 128] = n_in.reshape(128, 128).T
        m = {"xT": np.ascontiguousarray(xs.T), "w_in": w_in, "w_br": w_br, "w_out": w_out, "w_up": w_up,
             "w_down": w_down, "pp": pp, "cc": cc,
             "sg_S": np.ascontiguousarray(inp['state_gdn_S'][:, sl], f32),
             "sm_S": np.ascontiguousarray(inp['state_mlstm_C'][:, sl], f32),
             "sh_S": np.ascontiguousarray(inp['state_hgrn_S'][:, sl], f32),
             "gcs": np.ascontiguousarray(np.asarray(inp['state_gdn_conv'], f32)[:, sl].reshape(DEPTH, NSQ * 3, 1536)),
             "fcs": np.ascontiguousarray(np.asarray(inp['state_ffn_conv'], f32)[:, sl].reshape(DEPTH, NSQ * 2, 2 * DFF))}
        in_maps.append(m)
    if 'nc' not in _NC_CACHE:
        _NC_CACHE['nc'] = build_nc()
    res = run_bass_kernel_spmd(_NC_CACHE['nc'], in_maps, core_ids=list(range(8)))
    R = res.results
    yT = [np.asarray(r["yT"]) for r in R]
    y_prompt = np.stack([y[:, :TP].T for y in yT], 0)
    y_sample = np.concatenate([y[:, TP:].T.reshape(NSQ, 4, D) for y in yT], 0)

    def pst(key):
        return np.stack([np.asarray(r[key]) for r in R], 1)

    def sst(key):
        return np.concatenate([np.asarray(r[key]) for r in R], 1)

    p_gdn_S, p_mC, p_hS = pst("pg_S"), pst("pm_S"), pst("ph_S")
    s_gdn_S, s_mC, s_hS = sst("og_S"), sst("om_S"), sst("oh_S")
    p_gconv = np.stack([np.asarray(r["pgc"]) for r in R], 1)
    s_gconv = np.concatenate([np.asarray(r["sgc"]).reshape(DEPTH, NSQ, 3, 1536) for r in R], 1)
    p_fconv = np.stack([np.asarray(r["pfc"]) for r in R], 1)
    s_fconv = np.concatenate([np.asarray(r["sfc"]).reshape(DEPTH, NSQ, 2, 2 * DFF) for r in R], 1)
    nst = [np.asarray(r["nst"]) for r in R]
    mst = [np.asarray(r["mst"]) for r in R]
    p_n = np.stack([n_[:, 0:8].T.reshape(DEPTH, 4, 128) for n_ in nst], 1)
    s_n = np.concatenate([n_[:, 8:].T.reshape(DEPTH, NSQ, 4, 128) for n_ in nst], 1)
    p_m = np.stack([m_[0, 0:8].reshape(DEPTH, 4) for m_ in mst], 1)
    s_m = np.concatenate([m_[0, 8:].reshape(DEPTH, NSQ, 4) for m_ in mst], 1)
    outs = (y_prompt, y_sample, p_gdn_S, p_gconv, p_mC, p_n, p_m, p_hS, p_fconv,
            s_gdn_S, s_gconv, s_mC, s_n, s_m, s_hS, s_fconv)
    return tuple(np.ascontiguousarray(o, dtype=f32) for o in outs)
```
